# Optimizing a Trainium2 kernel written in Bass

```python
import math
import jax, jax.numpy as jnp
from jax import lax
import numpy as np

D_MODEL = 1024
BATCH = 8
SEQ = 4096
DEPTH = 1

M_HEADS = 4
M_HEAD_DIM = 128
M_WIDTH = M_HEADS * M_HEAD_DIM
M_CHUNK = 64
A_HEADS = 8
A_NOPE = 64
A_ROPE = 32
A_V = 64
A_WIDTH = A_HEADS * A_V
Q_RANK = 384
KV_RANK = 256
ROPE_THETA = 10000.0
Q_BLOCK = 128
D_MIX = M_WIDTH + A_WIDTH
D_FF = ((8 * D_MODEL // 3 + 255) // 256) * 256
EPS = 1e-6
IN_SPLITS = (M_WIDTH, M_WIDTH, M_WIDTH, M_WIDTH, M_HEADS, M_HEADS, Q_RANK, KV_RANK, A_ROPE)
D_IN = sum(IN_SPLITS)

kernel_name = "hymba_mlstm_mla_swiglu"


def rmsnorm(x, w):
    xf = x.astype(jnp.float32)
    y = xf * lax.rsqrt(jnp.mean(xf * xf, axis=-1, keepdims=True) + EPS)
    return (y * w.astype(jnp.float32)).astype(x.dtype)


def apply_rope(x, positions):
    half = A_ROPE // 2
    inv_freq = ROPE_THETA ** (-jnp.arange(half, dtype=jnp.float32) / half)
    ang = positions.astype(jnp.float32)[:, :, None, None] * inv_freq
    cos, sin = jnp.cos(ang), jnp.sin(ang)
    xf = x.astype(jnp.float32)
    x1, x2 = xf[..., :half], xf[..., half:]
    return jnp.concatenate([x1 * cos - x2 * sin, x2 * cos + x1 * sin], axis=-1).astype(x.dtype)


def mlstm_chunkwise(q, k, v, i_raw, f_raw):
    B, S, H, Dh = q.shape
    L = M_CHUNK
    NC = S // L
    f32 = jnp.float32

    def to_chunks(t):
        return t.astype(f32).reshape(B, NC, L, H, -1).transpose(0, 3, 1, 2, 4)

    def gate_chunks(t):
        return t.astype(f32).reshape(B, NC, L, H).transpose(0, 3, 1, 2)

    qc = to_chunks(q) * (Dh ** -0.5)
    kc = to_chunks(k)
    vc = to_chunks(v)
    logi = gate_chunks(i_raw)
    logf = jax.nn.log_sigmoid(gate_chunks(f_raw))
    b = jnp.cumsum(logf, axis=-1)
    g = b[..., -1]

    a = g[..., None] - b + logi
    m_loc = jnp.max(a, axis=-1)
    w = jnp.exp(a - m_loc[..., None])
    C_loc = jnp.einsum('bhcl,bhcld,bhcle->bhcde', w, kc, vc)
    n_loc = jnp.einsum('bhcl,bhcld->bhcd', w, kc)

    def step(carry, xs):
        C, n, m = carry
        g_c, m_l, C_l, n_l = xs
        m_new = jnp.maximum(g_c + m, m_l)
        s_old = jnp.exp(g_c + m - m_new)
        s_loc = jnp.exp(m_l - m_new)
        C_new = s_old[..., None, None] * C + s_loc[..., None, None] * C_l
        n_new = s_old[..., None] * n + s_loc[..., None] * n_l
        return (C_new, n_new, m_new), (C, n, m)

    init = (jnp.zeros((B, H, Dh, vc.shape[-1]), f32),
            jnp.zeros((B, H, Dh), f32),
            jnp.zeros((B, H), f32))
    xs = (jnp.moveaxis(g, 2, 0), jnp.moveaxis(m_loc, 2, 0),
          jnp.moveaxis(C_loc, 2, 0), jnp.moveaxis(n_loc, 2, 0))
    _, (C_prev, n_prev, m_prev) = lax.scan(step, init, xs)
    C_prev = jnp.moveaxis(C_prev, 0, 2)
    n_prev = jnp.moveaxis(n_prev, 0, 2)
    m_prev = jnp.moveaxis(m_prev, 0, 2)

    causal = jnp.tril(jnp.ones((L, L), dtype=bool))
    d_log = b[..., :, None] - b[..., None, :] + logi[..., None, :]
    d_log = jnp.where(causal, d_log, -jnp.inf)
    inter_log = b + m_prev[..., None]
    m_t = jnp.maximum(jnp.max(d_log, axis=-1), inter_log)
    p = jnp.exp(d_log - m_t[..., None]) * jnp.einsum('bhcld,bhcsd->bhcls', qc, kc)
    s_inter = jnp.exp(inter_log - m_t)
    num = (jnp.einsum('bhcls,bhcse->bhcle', p, vc)
           + s_inter[..., None] * jnp.einsum('bhcld,bhcde->bhcle', qc, C_prev))
    den = jnp.sum(p, axis=-1) + s_inter * jnp.einsum('bhcld,bhcd->bhcl', qc, n_prev)
    den = jnp.maximum(jnp.abs(den), jnp.exp(-m_t))
    h = num / den[..., None]
    return h.transpose(0, 2, 3, 1, 4).reshape(B, S, H, -1).astype(q.dtype)


def mla_attention(q_nope, q_rope, k_nope, k_rope, v):
    B, S, H, _ = q_nope.shape
    NQ = S // Q_BLOCK
    scale = (A_NOPE + A_ROPE) ** -0.5
    qn = q_nope.reshape(B, NQ, Q_BLOCK, H, A_NOPE).swapaxes(0, 1)
    qr = q_rope.reshape(B, NQ, Q_BLOCK, H, A_ROPE).swapaxes(0, 1)
    kr = k_rope[:, :, 0, :]
    k_pos = jnp.arange(S)

    def block(args):
        idx, qn_b, qr_b = args
        s = (jnp.einsum('bqhd,bkhd->bhqk', qn_b, k_nope)
             + jnp.einsum('bqhd,bkd->bhqk', qr_b, kr)).astype(jnp.float32) * scale
        q_pos = idx * Q_BLOCK + jnp.arange(Q_BLOCK)
        s = jnp.where(k_pos[None, :] <= q_pos[:, None], s, -jnp.inf)
        p = jax.nn.softmax(s, axis=-1).astype(v.dtype)
        return jnp.einsum('bhqk,bkhd->bqhd', p, v)

    out = lax.map(block, (jnp.arange(NQ), qn, qr))
    return out.swapaxes(0, 1).reshape(B, S, H * A_V)


def setup_inputs(seed: int = 0) -> dict:
    key = jax.random.key(seed)
    ks = jax.random.split(key, 20)
    f32 = jnp.float32

    def nrm(k, shape, scale):
        return jax.random.normal(k, shape, f32) * scale

    def gain(k, shape):
        return 1.0 + 0.01 * jax.random.normal(k, shape, f32)

    x = jax.random.normal(ks[0], (BATCH, SEQ, D_MODEL), f32)
    offsets = jax.random.randint(ks[1], (BATCH, 1), 0, 1024, dtype=jnp.int32)
    positions = (jnp.arange(SEQ, dtype=jnp.int32)[None, :] + offsets).astype(jnp.int32)
    i_bias = 0.1 * jax.random.normal(ks[2], (DEPTH, M_HEADS), f32)
    f_bias = jnp.linspace(3.0, 6.0, M_HEADS, dtype=f32)[None, :] + 0.1 * jax.random.normal(ks[3], (DEPTH, M_HEADS), f32)
    return {
        "x": x,
        "positions": positions,
        "attn_norm_w": gain(ks[4], (DEPTH, D_MODEL)),
        "w_in": nrm(ks[5], (DEPTH, D_MODEL, D_IN), D_MODEL ** -0.5),
        "b_gates": jnp.concatenate([i_bias, f_bias], axis=-1),
        "mlstm_norm_w": gain(ks[6], (DEPTH, M_HEADS, M_HEAD_DIM)),
        "q_a_norm_w": gain(ks[7], (DEPTH, Q_RANK)),
        "w_q_b": nrm(ks[8], (DEPTH, Q_RANK, A_HEADS * (A_NOPE + A_ROPE)), Q_RANK ** -0.5),
        "kv_a_norm_w": gain(ks[9], (DEPTH, KV_RANK)),
        "w_kv_b": nrm(ks[10], (DEPTH, KV_RANK, A_HEADS * (A_NOPE + A_V)), KV_RANK ** -0.5),
        "w_out": nrm(ks[11], (DEPTH, D_MIX, D_MODEL), D_MIX ** -0.5),
        "ffn_norm_w": gain(ks[12], (DEPTH, D_MODEL)),
        "w_gate": nrm(ks[13], (DEPTH, D_MODEL, D_FF), D_MODEL ** -0.5),
        "w_up": nrm(ks[14], (DEPTH, D_MODEL, D_FF), D_MODEL ** -0.5),
        "w_down": nrm(ks[15], (DEPTH, D_FF, D_MODEL), D_FF ** -0.5),
        "final_norm_w": gain(ks[16], (D_MODEL,)),
    }


def reference(x, positions, attn_norm_w, w_in, b_gates, mlstm_norm_w, q_a_norm_w, w_q_b,
              kv_a_norm_w, w_kv_b, w_out, ffn_norm_w, w_gate, w_up, w_down, final_norm_w):
    B, S = x.shape[0], x.shape[1]
    split_points = np.cumsum(np.array(IN_SPLITS))[:-1].tolist()
    h = x
    for l in range(DEPTH):
        u = rmsnorm(h, attn_norm_w[l])
        proj = u @ w_in[l]
        mq, mk, mv, mo, mi, mf, qa, kva, kr = jnp.split(proj, split_points, axis=-1)

        mi = mi + b_gates[l, :M_HEADS]
        mf = mf + b_gates[l, M_HEADS:]
        to_heads = lambda t: t.reshape(B, S, M_HEADS, M_HEAD_DIM)
        hm = mlstm_chunkwise(to_heads(mq), to_heads(mk), to_heads(mv), mi, mf)
        hm = rmsnorm(hm, mlstm_norm_w[l]).reshape(B, S, M_WIDTH)
        hm = hm * jax.nn.sigmoid(mo)

        cq = (rmsnorm(qa, q_a_norm_w[l]) @ w_q_b[l]).reshape(B, S, A_HEADS, A_NOPE + A_ROPE)
        q_nope, q_rope = cq[..., :A_NOPE], apply_rope(cq[..., A_NOPE:], positions)
        ckv = (rmsnorm(kva, kv_a_norm_w[l]) @ w_kv_b[l]).reshape(B, S, A_HEADS, A_NOPE + A_V)
        k_nope, v = ckv[..., :A_NOPE], ckv[..., A_NOPE:]
        k_rope = apply_rope(kr[:, :, None, :], positions)
        ha = mla_attention(q_nope, q_rope, k_nope, k_rope, v)

        h = h + jnp.concatenate([hm, ha], axis=-1) @ w_out[l]

        u = rmsnorm(h, ffn_norm_w[l])
        h = h + (jax.nn.silu(u @ w_gate[l]) * (u @ w_up[l])) @ w_down[l]
    return rmsnorm(h, final_norm_w)
```

```python
import math
import numpy as np
import ml_dtypes
from contextlib import ExitStack
import concourse.bass as bass
import concourse.mybir as mybir
from concourse.bass_utils import run_bass_kernel_spmd

F32 = mybir.dt.float32
BF16 = mybir.dt.bfloat16
I32 = mybir.dt.int32
U8 = mybir.dt.uint8
AF = mybir.ActivationFunctionType
ALU = mybir.AluOpType
AX = mybir.AxisListType

S = 4096
D = 1024
DIN = 2728
DFF = 2816
NT = S // 128
QT = 2
NST = NT // QT
SQ = QT * 128
EPS = 1e-6
GEN = 20000


class Op:
    __slots__ = ("eng", "fn", "deps", "is_dma", "sig", "sem", "val", "prev", "eidx")

    def __init__(self, eng, fn, is_dma):
        self.eng = eng
        self.fn = fn
        self.is_dma = is_dma
        self.deps = []
        self.sig = False
        self.sem = None
        self.val = 0
        self.prev = None


class Prog:
    NDMA = 12
    WINDOW = 16
    STRICT = True

    def __init__(self):
        self.ops = []
        self.last_w = {}
        self.readers = {}
        self.last_eng = {}
        self.dma_recent = {}
        self.pending_barrier = {}
        self.ecount = {}
        self.t_last_w = {}
        self.t_readers = {}

    def add(self, eng, fn, r=(), w=(), dma=False, sr=()):
        op = Op(eng, fn, dma)
        deps = []
        seen = set()
        isb = lambda k: len(k) >= 2 and k[0] == "B" and k[1].isdigit()
        nb = lambda k: k[:2] if isb(k) else k
        tr = list(dict.fromkeys([nb(k) for k in list(r) + list(sr)]))
        tw = list(dict.fromkeys([nb(k) for k in w]))
        w = list(dict.fromkeys([nb(k) for k in list(w) + [k for k in r if isb(k)]]))
        r = [k for k in r if not isb(k)] + list(sr)
        eidx = self.ecount.get(eng, 0)
        self.ecount[eng] = eidx + 1
        op.eidx = eidx

        def push(d, raw):
            if id(d) in seen:
                return
            if d.is_dma or d.eng != eng or dma:
                seen.add(id(d))
                deps.append(d)
            elif raw and eng != "pe" and eidx - d.eidx <= self.WINDOW:
                seen.add(id(d))
                deps.append(d)

        for k in r:
            d = self.last_w.get(k)
            if d is not None:
                push(d, True)
        for k in w:
            d = self.last_w.get(k)
            if d is not None:
                push(d, isb(k) and k in tr)
            for d in self.readers.get(k, ()):
                push(d, False)
        if self.STRICT:
            def spush(d):
                if id(d) not in seen and d.eng == eng and not d.is_dma and not (eng == "pe"):
                    seen.add(id(d))
                    deps.append(d)
            for k in tr:
                d = self.t_last_w.get(k)
                if d is not None:
                    spush(d)
            for k in tw:
                d = self.t_last_w.get(k)
                if d is not None:
                    spush(d)
                for d in self.t_readers.get(k, ()):
                    spush(d)
        for k in tw:
            self.t_last_w[k] = op
            self.t_readers[k] = []
        for k in tr:
            self.t_readers.setdefault(k, []).append(op)
        if eng in self.pending_barrier:
            for d in self.pending_barrier.pop(eng):
                push(d, False)
        for k in w:
            self.last_w[k] = op
            self.readers[k] = []
        for k in r:
            self.readers.setdefault(k, []).append(op)
        op.deps = deps
        for d in deps:
            d.sig = True
        self.ops.append(op)
        self.last_eng[eng] = op
        if dma:
            lst = self.dma_recent.setdefault(eng, [])
            lst.append(op)
            if len(lst) > self.NDMA:
                lst.pop(0)
        return op

    def barrier(self):
        deps = [o for o in self.last_eng.values() if not o.is_dma]
        for lst in self.dma_recent.values():
            deps.extend(lst)
        for e in ("pe", "act", "dve", "pool", "sp"):
            self.pending_barrier[e] = list(deps)

    def emit(self, nc, stack):
        names = {"pe": "tensor", "act": "scalar", "dve": "vector", "pool": "gpsimd", "sp": "sync"}
        per = {e: [o for o in self.ops if o.eng == e] for e in names}
        sems = {}

        def getsem(name):
            if name not in sems:
                sems[name] = stack.enter_context(nc.semaphore(name))
            return sems[name]

        for e, lst in per.items():
            cnt = 0
            dcnt = 0
            slot_last = {}
            for o in lst:
                if o.is_dma:
                    slot = dcnt % self.NDMA
                    o.sem = getsem(f"d_{e}_{slot}")
                    o.prev = slot_last.get(slot)
                    o.val = (o.prev.val if o.prev is not None else 0) + 16
                    slot_last[slot] = o
                    dcnt += 1
                elif o.sig:
                    o.sem = getsem(f"c_{e}_{cnt // GEN}")
                    o.val = cnt % GEN + 1
                    cnt += 1
        block = stack.enter_context(nc.Block())

        def run(e, lst):
            def body(eng):
                waited = {}
                for o in lst:
                    ws = [(d.sem, d.val) for d in o.deps]
                    if o.is_dma and o.prev is not None:
                        ws.append((o.prev.sem, o.prev.val))
                    for s, v in ws:
                        if waited.get(s.name, 0) < v:
                            eng.wait_ge(s, v)
                            waited[s.name] = v
                    ins = o.fn(eng)
                    if o.is_dma:
                        ins.then_inc(o.sem, 16)
                    elif o.sig:
                        ins.then_inc(o.sem, 1)
                last = {}
                for o in lst:
                    if o.is_dma:
                        last[o.sem.name] = (o.sem, o.val)
                for s, v in last.values():
                    if waited.get(s.name, 0) < v:
                        eng.wait_ge(s, v)
            return body

        for e, lst in per.items():
            if lst:
                getattr(block, names[e])(run(e, lst))


def build_program(nst=NST, do_p2=True, dbg=False, lvl=99):
    nc = bass.Bass("TRN2", target_bir_lowering=False, dynamic_dma_scratch_size=256)
    dt = lambda n, s, d, k="ExternalInput": nc.dram_tensor(n, s, d, kind=k).ap()
    x_d = dt("x", [S, D], F32)
    pos_d = dt("pos", [128, NT], I32)
    w_in_d = dt("w_in", [D, DIN], F32)
    wqb_d = dt("w_q_b", [384, 768], F32)
    wkvb_d = dt("w_kv_b", [256, 1024], F32)
    wout_d = dt("w_out", [D, D], F32)
    wg_d = dt("w_gate", [D, DFF], F32)
    wu_d = dt("w_up", [D, DFF], F32)
    wd_d = dt("w_down", [DFF, D], F32)
    gains_d = dt("gains", [128, 25], F32)
    fnw_d = dt("final_nw", [1, D], F32)
    bg_d = dt("b_gates", [1, 8], F32)
    identb_d = dt("identb", [128, 128], BF16)
    cf_d = dt("cf", [128, 384], F32)
    out_d = dt("out", [S, D], F32, "ExternalOutput")
    mix_d = dt("mixs", [S, D], BF16, "Internal")
    dbg_d = dt("dbg", [S, D], BF16, "ExternalOutput") if dbg else None
    dbgf_d = dt("dbgf", [128, 1024], F32, "ExternalOutput") if dbg else None

    P = Prog()
    inv_freq = (np.float32(10000.0) ** (-np.arange(16, dtype=np.float32) / np.float32(16))).astype(np.float32)

    with ExitStack() as st:
        ARENA = 222 * 1024
        arena = st.enter_context(nc.sbuf_tensor("arena", [128, ARENA], U8))
        banks = [st.enter_context(nc.psum_tensor(f"bank{i}", [128, 512], F32)) for i in range(8)]
        off = [0]

        def alloc(shape, dtype, at=None):
            nb = int(np.prod(shape[1:])) * mybir.dt.size(dtype)
            nb = (nb + 63) // 64 * 64
            o = off[0] if at is None else at
            if at is None:
                off[0] += nb
            assert o + nb <= ARENA, (o, nb)
            ap = arena[:, o:o + nb].bitcast(dtype)
            n = int(np.prod(shape[1:]))
            ap = ap[:, 0:n]
            if len(shape) == 3:
                ap = ap.rearrange("p (a b) -> p a b", a=shape[1])
            elif len(shape) == 4:
                ap = ap.rearrange("p (a b c) -> p a b c", a=shape[1], b=shape[2])
            return ap

        def pb(i):
            return banks[i][:]

        def pbb(i):
            return banks[i][:].bitcast(BF16)

        identb = alloc([128, 128], BF16)
        cf = alloc([128, 384], F32)
        identf = cf[:, 0:128]
        tri = cf[:, 128:256]
        ones = cf[:, 256:384]
        gains = alloc([128, 25], F32)
        epst = alloc([128, 1], F32)
        onec = alloc([128, 1], F32)
        ssq = alloc([128, 4], F32)
        rstd = alloc([128, 4], F32)
        rstd2 = alloc([128, 4], F32)
        common_end = off[0]
        junk_at = off[0]
        junk = alloc([128, 1024], BF16)

        kT_at = off[0]
        kT = alloc([128, 8, S], BF16)
        vc_at = off[0]
        vc = alloc([128, NT, 8, 65], BF16)
        w_in = alloc([128, 8, DIN], BF16)
        wqb = alloc([128, 3, 768], BF16)
        wkvb = alloc([128, 2, 1024], BF16)
        cosT = alloc([128, NT, 16], F32)
        sinT = alloc([128, NT, 16], F32)
        bgate = alloc([128, 8], F32)
        tri_b = alloc([128, 128], BF16)
        Cst = alloc([128, 4, 129], F32)
        mprev = [alloc([128, 4], F32) for _ in range(2)]
        vext = [alloc([128, 4, 129], BF16) for _ in range(3)]
        p1_misc = off[0]
        stage_at = off[0]
        xring = [alloc([128, D], F32) for _ in range(2)]
        u_t = alloc([128, D], BF16)
        uT = alloc([128, 8, 128], BF16)
        q_tm = [alloc([128, 512], BF16) for _ in range(2)]
        qT4 = [alloc([128, 4, 128], BF16) for _ in range(3)]
        k_tm = [alloc([128, 512], BF16) for _ in range(3)]
        og = [alloc([128, 512], BF16) for _ in range(3)]
        ogt = alloc([128, 512], F32, at=junk_at)
        lat = [alloc([128, 680], F32) for _ in range(2)]
        gts = [alloc([128, 8], F32) for _ in range(2)]
        ef = [alloc([128, 4], F32) for _ in range(2)]
        spl = [alloc([128, 12], F32) for _ in range(2)]
        apr = [alloc([128, 4], F32) for _ in range(2)]
        cst_ = [alloc([128, 4], F32) for _ in range(2)]
        Mx = [alloc([128, 4], F32) for _ in range(2)]
        args = [alloc([128, 12], F32) for _ in range(2)]
        exs = [alloc([128, 12], F32) for _ in range(2)]
        ke = [alloc([128, 128], BF16) for _ in range(2)]
        keT = [alloc([128, 128], BF16) for _ in range(2)]
        PTm = [alloc([128, 128], BF16) for _ in range(2)]
        Cb = [alloc([128, 129], BF16) for _ in range(2)]
        sm = [alloc([128, 8], F32) for _ in range(2)]
        junk2 = alloc([128, 128], BF16)
        junk3 = alloc([128, 384], BF16)
        rk = [alloc([128, 16], F32) for _ in range(4)]
        lan = alloc([128, 640], BF16)
        lanT = alloc([128, 5, 128], BF16)
        qtm = alloc([128, 8, 96], BF16)
        ktm = alloc([128, 8, 96], BF16)
        rt = [alloc([128, 8, 16], F32) for _ in range(4)]
        kro = alloc([128, 32], BF16)
        qTall2 = [alloc([128, 8, SQ], BF16) for _ in range(2)]
        PT = [alloc([128, 2 * SQ], BF16) for _ in range(3)]
        oT = alloc([128, SQ], F32)
        mix2 = [alloc([128, QT, D], BF16) for _ in range(2)]
        rd = alloc([128, 1], F32)
        p1_end = off[0]
        VT0 = 10
        stg = [alloc([128, DFF], F32, at=vc_at + VT0 * 1040 + i * 11264) for i in range(2)]
        assert VT0 * 1040 + 2 * 11264 <= NT * 1040
        posi = alloc([128, NT], I32, at=kT_at)
        posf = alloc([128, NT], F32, at=kT_at + 128)
        ang = alloc([128, NT, 32], F32, at=kT_at + 256)
        ki = alloc([128, NT * 32], I32, at=kT_at + 256 + 4096)
        kf = alloc([128, NT * 32], F32, at=kT_at + 256 + 8192)

        off[0] = common_end
        wg = alloc([128, 8, DFF], BF16)
        wu = alloc([128, 8, DFF], BF16)
        wd = alloc([128, 22, D], BF16)
        wout = alloc([128, 8, D], BF16)
        fnw = alloc([128, D], F32)
        h1t = alloc([128, 4, D], F32)
        hn2 = [alloc([128, D], BF16) for _ in range(2)]
        hn = hn2[0]
        mixT2 = [alloc([128, 8, 128], BF16) for _ in range(2)]
        hnT = alloc([128, 8, 512], BF16)
        sg_ = alloc([128, 512], BF16)
        sg = [sg_, sg_]
        act_at = off[0]
        actT = alloc([128, 22, 512], BF16)
        stg2 = [alloc([128, 1408], F32) for i in range(2)]
        p2_end = off[0]

        cur = [None]
        atom = [0]

        def A(eng, fn, r=(), w=(), dma=False, sr=()):
            if cur[0] is not None:
                op = (eng, fn, tuple(r), tuple(w), dma, tuple(sr))
                if atom[0] and cur[0] and cur[0][-1][0] == "open":
                    cur[0][-1][1].append(op)
                elif atom[0]:
                    cur[0].append(["open", [op]])
                else:
                    cur[0].append(["seg", [op]])
            else:
                P.add(eng, fn, r=r, w=w, dma=dma, sr=sr)

        class atomic:
            def __enter__(self):
                atom[0] += 1

            def __exit__(self, *a_):
                atom[0] -= 1
                if atom[0] == 0 and cur[0] and cur[0][-1][0] == "open":
                    cur[0][-1][0] = "seg"

        def record(f, *args):
            cur[0] = []
            f(*args)
            lst = [seg[1] for seg in cur[0]]
            cur[0] = None
            return lst

        def feed(segs):
            for seg in segs:
                for o in seg:
                    P.add(o[0], o[1], r=o[2], w=o[3], dma=o[4], sr=o[5])

        def merge(l1, l2):
            n1 = sum(len(x) for x in l1)
            n2 = sum(len(x) for x in l2)
            out = []
            i = j = 0
            c1 = c2 = 0
            while i < len(l1) or j < len(l2):
                if j >= len(l2) or (i < len(l1) and (c1 + 0.5 * len(l1[i])) * n2 <= (c2 + 0.5 * len(l2[j])) * n1):
                    out.append(l1[i]); c1 += len(l1[i]); i += 1
                else:
                    out.append(l2[j]); c2 += len(l2[j]); j += 1
            return out

        cast_rr = [0]

        def cast(out, in_, scale, r, w):
            i = cast_rr[0]
            cast_rr[0] += 1
            rr = list(r) + (["gains"] if scale is not None else [])
            if i % 2 == 0:
                if scale is None:
                    A("act", lambda e: e.activation(out=out, in_=in_, func=AF.Copy), r=rr, w=w)
                else:
                    A("act", lambda e: e.activation(out=out, in_=in_, func=AF.Identity, scale=scale), r=rr, w=w)
            else:
                if scale is None:
                    A("dve", lambda e: e.tensor_copy(out=out, in_=in_), r=rr, w=w)
                else:
                    A("dve", lambda e: e.tensor_scalar(out=out, in0=in_, scalar1=scale, scalar2=None, op0=ALU.mult), r=rr, w=w)

        def dma(out, in_, r=(), w=()):
            A("sp", lambda e: e.dma_start(out=out, in_=in_), r=r, w=w, dma=True)

        def dump(ap, c0, n, rk):
            if dbg:
                npart = ap.shape[0]
                A("sp", lambda e: e.dma_start(out=dbgf_d[0:npart, c0:c0 + n], in_=ap, allow_slow_non_contiguous=True), r=rk, w=[f"dbgf{c0}"], dma=True)

        def evac(i, out, in_, r, w):
            if i % 2 == 0:
                A("act", lambda e: e.activation(out=out, in_=in_, func=AF.Copy), r=r, w=w)
            else:
                A("dve", lambda e: e.tensor_copy(out=out, in_=in_), r=r, w=w)

        dma(identb, identb_d[:, :], w=["identb"])
        dma(cf, cf_d[:, :], w=["cf"])
        dma(gains, gains_d[:, :], w=["gains"])
        dma(bgate, bg_d.partition_broadcast(128), w=["bgate"])
        dma(posi, pos_d[:, :], w=["posi"])
        A("dve", lambda e: e.memset(epst, EPS), w=["epst"])
        A("dve", lambda e: e.memset(onec, 1.0), w=["onec"])
        A("dve", lambda e: e.tensor_copy(out=tri_b, in_=tri), r=["cf"], w=["tri_b"])
        A("dve", lambda e: e.memset(Cst.rearrange("p a b -> p (a b)"), 0.0), w=["C0", "C1", "C2", "C3"])
        A("dve", lambda e: e.memset(mprev[0], 0.0), w=["mprev0"])
        A("pool", lambda e: e.memset(vc[:, 0:VT0].rearrange("p a b c -> p (a b c)"), 1.0), w=[f"vc{t}" for t in range(VT0)])
        for b in range(3):
            A("pool", lambda e, b=b: e.memset(vext[b].rearrange("p a b -> p (a b)"), 1.0), w=[f"vext{b}"])

        A("dve", lambda e: e.tensor_copy(out=posf, in_=posi), r=["posi"], w=["posf"])
        for j in range(16):
            A("dve", lambda e, j=j: e.tensor_scalar(out=ang[:, :, j], in0=posf, scalar1=float(inv_freq[j]), scalar2=None, op0=ALU.mult),
              r=["posf"], w=["ang"])
        A("dve", lambda e: e.tensor_scalar(out=ang[:, :, 16:32], in0=ang[:, :, 0:16], scalar1=float(np.pi / 2), scalar2=None, op0=ALU.add),
          r=["ang"], w=["ang"])
        angf = ang.rearrange("p a b -> p (a b)")
        TWO_PI = float(2 * np.pi)
        A("dve", lambda e: e.tensor_scalar(out=ki, in0=angf, scalar1=float(1.0 / TWO_PI), scalar2=None, op0=ALU.mult), r=["ang"], w=["ki"])
        A("dve", lambda e: e.tensor_copy(out=kf, in_=ki), r=["ki"], w=["kf"])
        A("dve", lambda e: e.scalar_tensor_tensor(out=angf, in0=kf, scalar=-TWO_PI, in1=angf, op0=ALU.mult, op1=ALU.add), r=["kf", "ang"], w=["ang"])
        A("dve", lambda e: e.tensor_scalar(out=kf, in0=angf, scalar1=float(np.pi), scalar2=-TWO_PI, op0=ALU.is_gt, op1=ALU.mult), r=["ang"], w=["kf"])
        A("dve", lambda e: e.tensor_tensor(out=angf, in0=angf, in1=kf, op=ALU.add), r=["ang", "kf"], w=["ang"])
        A("dve", lambda e: e.tensor_scalar(out=kf, in0=angf, scalar1=float(-np.pi), scalar2=TWO_PI, op0=ALU.is_lt, op1=ALU.mult), r=["ang"], w=["kf"])
        A("dve", lambda e: e.tensor_tensor(out=angf, in0=angf, in1=kf, op=ALU.add), r=["ang", "kf"], w=["ang"])
        A("dve", lambda e: e.tensor_scalar(out=angf, in0=angf, scalar1=3.14159, scalar2=-3.14159, op0=ALU.min, op1=ALU.max), r=["ang"], w=["ang"])
        A("act", lambda e: e.activation(out=sinT, in_=ang[:, :, 0:16], func=AF.Sin), r=["ang"], w=["sinT"])
        A("act", lambda e: e.activation(out=cosT, in_=ang[:, :, 16:32], func=AF.Sin), r=["ang"], w=["cosT"])

        def load_w(dst, src_d, nk, ncols, gcol0, key, stgs, skey):
            for kc in range(nk):
                s = stgs[kc % 2][:, 0:ncols]
                sk = f"{skey}{kc % 2}"
                dma(s, src_d[kc * 128:(kc + 1) * 128, :], w=[sk])
                sc = None if gcol0 is None or (gcol0 == 13 and kc >= 4) else gains[:, gcol0 + kc:gcol0 + kc + 1]
                cast(dst[:, kc, :], s, sc, r=[sk], w=[f"{key}{kc}"])

        load_w(w_in, w_in_d, 8, DIN, 0, "w_in", stg, "stg")
        load_w(wqb, wqb_d, 3, 768, 8, "wqb", stg, "stg")
        load_w(wkvb, wkvb_d, 2, 1024, 11, "wkvb", stg, "stg")
        A("pool", lambda e: e.memset(vc[:, VT0:NT].rearrange("p a b c -> p (a b c)"), 1.0), w=["stg0", "stg1"] + [f"vc{t}" for t in range(VT0, NT)])

        WIN = [f"w_in{k}" for k in range(8)]
        SC_ATT = float(96.0 ** -0.5)

        def P_(t):
            tp = t % 2
            t3 = t % 3
            xt = xring[tp]
            xk = f"x{tp}"
            dma(xt, x_d[t * 128:(t + 1) * 128, :], w=[xk])
            A("act", lambda e: e.activation(out=junk, in_=xt, func=AF.Square, scale=float(D ** -0.5), accum_out=ssq[:, 0:1]),
              r=[xk], w=["junk", "ssq"])
            A("act", lambda e: e.activation(out=rstd[:, 3:4], in_=ssq[:, 0:1], func=AF.Ln, bias=epst), r=["ssq"], sr=["epst"], w=["rstdl"])
            A("act", lambda e: e.activation(out=rstd[:, 0:1], in_=rstd[:, 3:4], func=AF.Exp, scale=-0.5), r=["rstdl"], w=["rstd"])
            A("dve", lambda e: e.tensor_scalar(out=u_t, in0=xt, scalar1=rstd[:, 0:1], scalar2=None, op0=ALU.mult), r=[xk], sr=["rstd"], w=["u"])
            with atomic():
                for kc in range(8):
                    A("pe", lambda e, kc=kc: e.transpose(out=pbb(0)[:, kc * 128:(kc + 1) * 128], in_=u_t[:, kc * 128:(kc + 1) * 128], identity=identb),
                      r=["u", "identb"], w=["B0"])
                A("dve", lambda e: e.tensor_copy(out=uT.rearrange("p a b -> p (a b)"), in_=pbb(0)), r=["B0"], w=["uT"])

            def proj(bank, c0, cn):
                for kc in range(8):
                    A("pe", lambda e, kc=kc: e.matmul(pb(bank)[:, 0:cn], lhsT=uT[:, kc, :], rhs=w_in[:, kc, c0:c0 + cn], start=(kc == 0), stop=(kc == 7)),
                      r=["uT", WIN[kc]], w=[f"B{bank}"])

            vb = vext[t3]
            with atomic():
                proj(2, 0, 512)
                A("act", lambda e: e.activation(out=q_tm[tp], in_=pb(2), func=AF.Copy, scale=float(128.0 ** -0.5)), r=["B2"], w=[f"q_tm{tp}"])
            with atomic():
                proj(3, 512, 512)
                A("dve", lambda e: e.tensor_copy(out=k_tm[t3], in_=pb(3)), r=["B3"], w=[f"k_tm{t3}"])
            with atomic():
                proj(2, 1024, 512)
                A("act", lambda e: e.activation(out=vb[:, :, 0:128], in_=pb(2).rearrange("p (a b) -> p a b", a=4), func=AF.Copy), r=["B2"], w=[f"vext{t3}"])
            with atomic():
                proj(3, 1536, 512)
                A("act", lambda e: e.activation(out=ogt, in_=pb(3), func=AF.Exp, scale=-1.0), r=["B3"], w=["junk"])
            A("dve", lambda e: e.tensor_scalar(out=ogt, in0=ogt, scalar1=1.0, scalar2=None, op0=ALU.add), r=["junk"], w=["junk"])
            def _rcp(e):
                with nc.allow_low_precision("output gate is stored in bf16 (it only scales the bf16 mixer output)"):
                    return e.reciprocal(out=og[t3], in_=ogt)
            A("dve", _rcp, r=["junk"], w=[f"og{t3}"])
            with atomic():
                proj(2, 2048, 512)
                A("dve", lambda e: e.tensor_copy(out=lat[tp][:, 0:512], in_=pb(2)), r=["B2"], w=[f"lat{tp}"])
            with atomic():
                proj(3, 2560, 168)
                A("dve", lambda e: e.tensor_copy(out=lat[tp][:, 512:680], in_=pb(3)[:, 0:168]), r=["B3"], w=[f"lat{tp}"])
            with atomic():
                for h in range(4):
                    A("pe", lambda e, h=h: e.transpose(out=pbb(0)[:, h * 128:(h + 1) * 128], in_=q_tm[tp][:, h * 128:(h + 1) * 128], identity=identb),
                      r=[f"q_tm{tp}", "identb"], w=["B0"])
                A("dve", lambda e: e.tensor_copy(out=qT4[t3].rearrange("p a b -> p (a b)"), in_=pbb(0)[:, 0:512]), r=["B0"], w=[f"qT4{t3}"])

        def G_(t):
            tp = t % 2
            latt = lat[tp]
            g_, sp, ap_, mx, ar, ex, cc = gts[tp], spl[tp], apr[tp], Mx[tp], args[tp], exs[tp], cst_[tp]
            mp = mprev[tp]
            mn = mprev[1 - tp]
            mpk = f"mprev{tp}"
            mnk = f"mprev{1 - tp}"
            gk = f"g{tp}"
            A("dve", lambda e: e.tensor_tensor(out=g_, in0=latt[:, 0:8], in1=bgate, op=ALU.add), r=[f"lat{tp}", "bgate"], w=[gk + "gts"])
            A("act", lambda e: e.activation(out=ef[tp], in_=g_[:, 4:8], func=AF.Exp, scale=-1.0), r=[gk + "gts"], w=[gk + "ef"])
            A("act", lambda e: e.activation(out=g_[:, 4:8], in_=ef[tp], func=AF.Ln, bias=onec), r=[gk + "ef", "onec"], w=[gk + "gts"])
            with atomic():
                A("pe", lambda e: e.matmul(pb(0)[:, 0:4], lhsT=tri, rhs=g_[:, 4:8], start=True, stop=True), r=["cf", gk + "gts"], w=["B0"])
                A("pe", lambda e: e.matmul(pb(0)[:, 4:12], lhsT=ones, rhs=g_, start=True, stop=True), r=["cf", gk + "gts"], w=["B0"])
                A("dve", lambda e: e.tensor_copy(out=sp, in_=pb(0)[:, 0:12]), r=["B0"], w=[gk + "sp"])
            A("dve", lambda e: e.tensor_tensor(out=ap_, in0=sp[:, 0:4], in1=g_[:, 0:4], op=ALU.add), r=[gk + "sp", gk + "gts"], w=[gk + "apr"])
            A("dve", lambda e: e.tensor_scalar(out=cc, in0=sp[:, 8:12], scalar1=0.5, scalar2=None, op0=ALU.mult), r=[gk + "sp"], w=[gk + "cc"])
            A("dve", lambda e: e.scalar_tensor_tensor(out=cc, in0=sp[:, 4:8], scalar=float(1.0 / 128), in1=cc, op0=ALU.mult, op1=ALU.add), r=[gk + "sp", gk + "cc"], w=[gk + "cc"])
            A("dve", lambda e: e.tensor_tensor(out=mx, in0=cc, in1=mp, op=ALU.max), r=[gk + "cc", mpk], w=[gk + "Mx"])
            A("dve", lambda e: e.tensor_tensor(out=mn, in0=mx, in1=sp[:, 8:12], op=ALU.subtract), r=[gk + "sp", gk + "Mx"], w=[mnk])
            A("dve", lambda e: e.tensor_tensor(out=ar[:, 0:4], in0=ap_, in1=mx, op=ALU.subtract), r=[gk + "apr", gk + "Mx"], w=[gk + "args"])
            A("dve", lambda e: e.tensor_tensor(out=ar[:, 4:8], in0=mp, in1=mx, op=ALU.subtract), r=[mpk, gk + "Mx"], w=[gk + "args"])
            A("dve", lambda e: e.tensor_tensor(out=ar[:, 8:12], in0=sp[:, 0:4], in1=mx, op=ALU.subtract), r=[gk + "sp", gk + "Mx"], w=[gk + "args"])
            A("act", lambda e: e.activation(out=ex, in_=ar, func=AF.Exp), r=[gk + "args"], w=[gk + "exs"])

        def H_(t, par):
            tp = t % 2
            t3 = t % 3
            sp_ = (t // QT) % 2
            mix = mix2[sp_]
            sub = t % QT
            vb = vext[t3]
            vk = f"vext{t3}"
            ex = exs[tp]
            ek = f"g{tp}exs"
            for h in (par, par + 2):
                b = h % 2
                bank = 1 if b == 0 else 7
                bk = f"B{bank}"
                hs = slice(h * 128, (h + 1) * 128)
                A("dve", lambda e, h=h, b=b, hs=hs: e.tensor_scalar(out=ke[b], in0=k_tm[t3][:, hs], scalar1=ex[:, h:h + 1], scalar2=None, op0=ALU.mult),
                  r=[f"k_tm{t3}"], sr=[ek], w=[f"ke{b}"])
                A("pe", lambda e, b=b, bank=bank: e.transpose(out=pbb(bank)[:, 776:904], in_=ke[b], identity=identb), r=[f"ke{b}", "identb"], w=[bk])
                A("dve", lambda e, b=b, bank=bank: e.tensor_copy(out=keT[b], in_=pbb(bank)[:, 776:904]), r=[bk], w=[f"keT{b}"])
                A("pe", lambda e, b=b, bank=bank, h=h: e.matmul(pb(bank)[:, 0:128], lhsT=keT[b], rhs=qT4[t3][:, h, :], start=True, stop=True), r=[f"keT{b}", f"qT4{t3}"], w=[bk])
                A("dve", lambda e, b=b, bank=bank: e.tensor_tensor(out=PTm[b], in0=pb(bank)[:, 0:128], in1=tri, op=ALU.mult), r=[bk, "cf"], w=[f"PTm{b}"])
                A("dve", lambda e, h=h: e.tensor_scalar(out=Cst[:, h, :], in0=Cst[:, h, :], scalar1=ex[:, 4 + h:5 + h], scalar2=None, op0=ALU.mult), r=[f"C{h}"], sr=[ek], w=[f"C{h}"])
                A("pool", lambda e, h=h, b=b: e.tensor_copy(out=Cb[b], in_=Cst[:, h, :]), r=[f"C{h}"], w=[f"Cb{b}"])
                A("pe", lambda e, b=b, bank=bank, h=h: e.matmul(pb(bank)[:, 128:257], lhsT=PTm[b], rhs=vb[:, h, :], start=True, stop=False), r=[f"PTm{b}", vk], w=[bk])
                A("pe", lambda e, b=b, bank=bank, h=h: e.matmul(pb(bank)[:, 128:257], lhsT=qT4[t3][:, h, :], rhs=Cb[b], start=False, stop=True), r=[f"qT4{t3}", f"Cb{b}"], w=[bk])
                A("pe", lambda e, b=b, bank=bank, h=h: e.matmul(pb(bank)[:, 258:387], lhsT=ke[b], rhs=vb[:, h, :], start=True, stop=True), r=[f"ke{b}", vk], w=[bk])
                nd = pb(bank)[:, 128:257]
                smb = sm[b]
                smk = f"sm{b}"
                A("act", lambda e, nd=nd, smb=smb: e.activation(out=smb[:, 0:1], in_=nd[:, 128:129], func=AF.Abs), r=[bk], w=[smk])
                A("act", lambda e, nd=nd, smb=smb, b=b: e.activation(out=junk2[:, 0:128], in_=nd[:, 0:128], func=AF.Square, scale=float(128.0 ** -0.5), accum_out=smb[:, 2:3]),
                  r=[bk], w=["junk2", smk])
                A("dve", lambda e, bank=bank, h=h: e.tensor_tensor(out=Cst[:, h, :], in0=pb(bank)[:, 258:387], in1=Cst[:, h, :], op=ALU.add), r=[bk, f"C{h}"], w=[f"C{h}"])
                A("dve", lambda e, h=h, smb=smb: e.tensor_tensor(out=smb[:, 0:1], in0=smb[:, 0:1], in1=ex[:, 8 + h:9 + h], op=ALU.max), r=[smk, ek], w=[smk])
                A("dve", lambda e, smb=smb: e.tensor_tensor(out=smb[:, 1:2], in0=smb[:, 0:1], in1=smb[:, 0:1], op=ALU.mult), r=[smk], w=[smk])
                A("dve", lambda e, smb=smb: e.scalar_tensor_tensor(out=smb[:, 3:4], in0=smb[:, 1:2], scalar=EPS, in1=smb[:, 2:3], op0=ALU.mult, op1=ALU.add), r=[smk], w=[smk])
                A("act", lambda e, smb=smb: e.activation(out=smb[:, 4:5], in_=smb[:, 3:4], func=AF.Ln), r=[smk], w=[smk])
                A("act", lambda e, smb=smb: e.activation(out=smb[:, 6:7], in_=smb[:, 4:5], func=AF.Exp, scale=-0.5), r=[smk], w=[smk])
                A("dve", lambda e, nd=nd, hs=hs, smb=smb: e.scalar_tensor_tensor(out=mix[:, sub, hs], in0=nd[:, 0:128], scalar=smb[:, 6:7], in1=og[t3][:, hs], op0=ALU.mult, op1=ALU.mult),
                  r=[bk, f"og{t3}"], sr=[smk], w=[f"mix{sp_}_{sub}"])

        def L_(t):
            tp = t % 2
            sp_ = (t // QT) % 2
            qTall = qTall2[sp_]
            sub = t % QT
            latt = lat[tp]
            lk = f"lat{tp}"
            A("act", lambda e: e.activation(out=junk3[:, 0:384], in_=latt[:, 8:392], func=AF.Square, scale=float(384.0 ** -0.5), accum_out=ssq[:, 1:2]), r=[lk], w=["junk3", "ssq1"])
            A("act", lambda e: e.activation(out=junk3[:, 0:256], in_=latt[:, 392:648], func=AF.Square, scale=float(256.0 ** -0.5), accum_out=ssq[:, 2:3]), r=[lk], w=["junk3", "ssq1"])
            A("act", lambda e: e.activation(out=rstd2[:, 0:2], in_=ssq[:, 1:3], func=AF.Ln, bias=epst), r=["ssq1", "epst"], w=["rstd1l"])
            A("act", lambda e: e.activation(out=rstd[:, 1:3], in_=rstd2[:, 0:2], func=AF.Exp, scale=-0.5), r=["rstd1l"], w=["rstd1"])
            A("dve", lambda e: e.tensor_scalar(out=lan[:, 0:384], in0=latt[:, 8:392], scalar1=rstd[:, 1:2], scalar2=None, op0=ALU.mult), r=[lk], sr=["rstd1"], w=["lan"])
            A("dve", lambda e: e.tensor_scalar(out=lan[:, 384:640], in0=latt[:, 392:648], scalar1=rstd[:, 2:3], scalar2=None, op0=ALU.mult), r=[lk], sr=["rstd1"], w=["lan"])
            with atomic():
                for j in range(5):
                    A("pe", lambda e, j=j: e.transpose(out=pbb(0)[:, j * 128:(j + 1) * 128], in_=lan[:, j * 128:(j + 1) * 128], identity=identb), r=["lan", "identb"], w=["B0"])
                A("act", lambda e: e.activation(out=lanT.rearrange("p a b -> p (a b)"), in_=pbb(0)[:, 0:640], func=AF.Copy), r=["B0"], w=["lanT"])
            atom[0] += 1
            for (bank, c0, cn) in ((2, 0, 480), (3, 480, 288)):
                for kc in range(3):
                    A("pe", lambda e, bank=bank, c0=c0, cn=cn, kc=kc: e.matmul(pb(bank)[:, 0:cn], lhsT=lanT[:, kc, :], rhs=wqb[:, kc, c0:c0 + cn], start=(kc == 0), stop=(kc == 2)),
                      r=["lanT", f"wqb{kc}"], w=[f"B{bank}"])
            cs = cosT[:, t, :]
            sn = sinT[:, t, :]
            for (bank, h0, nh) in ((2, 0, 5), (3, 5, 3)):
                v = pb(bank)[:, 0:nh * 96].rearrange("p (h c) -> p h c", h=nh)
                bk = f"B{bank}"
                csb = cs.unsqueeze(1).to_broadcast([128, nh, 16])
                snb = sn.unsqueeze(1).to_broadcast([128, nh, 16])
                A("act", lambda e, v=v, h0=h0, nh=nh: e.activation(out=qtm[:, h0:h0 + nh, 0:64], in_=v[:, :, 0:64], func=AF.Copy), r=[bk], w=["qtm"])
                A("dve", lambda e, v=v, nh=nh, csb=csb: e.tensor_tensor(out=rt[0][:, 0:nh, :], in0=v[:, :, 64:80], in1=csb, op=ALU.mult), r=[bk, "cosT"], w=["rt0"])
                A("dve", lambda e, v=v, nh=nh, snb=snb: e.tensor_tensor(out=rt[1][:, 0:nh, :], in0=v[:, :, 80:96], in1=snb, op=ALU.mult), r=[bk, "sinT"], w=["rt1"])
                A("dve", lambda e, v=v, nh=nh, csb=csb: e.tensor_tensor(out=rt[2][:, 0:nh, :], in0=v[:, :, 80:96], in1=csb, op=ALU.mult), r=[bk, "cosT"], w=["rt2"])
                A("dve", lambda e, v=v, nh=nh, snb=snb: e.tensor_tensor(out=rt[3][:, 0:nh, :], in0=v[:, :, 64:80], in1=snb, op=ALU.mult), r=[bk, "sinT"], w=["rt3"])
                A("pool", lambda e, h0=h0, nh=nh: e.tensor_tensor(out=qtm[:, h0:h0 + nh, 64:80], in0=rt[0][:, 0:nh, :], in1=rt[1][:, 0:nh, :], op=ALU.subtract), r=["rt0", "rt1"], w=["qtm"])
                A("pool", lambda e, h0=h0, nh=nh: e.tensor_tensor(out=qtm[:, h0:h0 + nh, 80:96], in0=rt[2][:, 0:nh, :], in1=rt[3][:, 0:nh, :], op=ALU.add), r=["rt2", "rt3"], w=["qtm"])
            atom[0] -= 1
            if cur[0] and cur[0][-1][0] == "open":
                cur[0][-1][0] = "seg"
            kr = latt[:, 648:680]
            A("pool", lambda e: e.tensor_tensor(out=rk[0], in0=kr[:, 0:16], in1=cs, op=ALU.mult), r=[lk, "cosT"], w=["rk0"])
            A("pool", lambda e: e.tensor_tensor(out=rk[1], in0=kr[:, 16:32], in1=sn, op=ALU.mult), r=[lk, "sinT"], w=["rk1"])
            A("pool", lambda e: e.tensor_tensor(out=rk[2], in0=kr[:, 16:32], in1=cs, op=ALU.mult), r=[lk, "cosT"], w=["rk2"])
            A("pool", lambda e: e.tensor_tensor(out=rk[3], in0=kr[:, 0:16], in1=sn, op=ALU.mult), r=[lk, "sinT"], w=["rk3"])
            A("pool", lambda e: e.tensor_tensor(out=kro[:, 0:16], in0=rk[0], in1=rk[1], op=ALU.subtract), r=["rk0", "rk1"], w=["kro"])
            A("pool", lambda e: e.tensor_tensor(out=kro[:, 16:32], in0=rk[2], in1=rk[3], op=ALU.add), r=["rk2", "rk3"], w=["kro"])
            A("pool", lambda e: e.tensor_copy(out=ktm[:, :, 64:96], in_=kro.unsqueeze(1).to_broadcast([128, 8, 32])), r=["kro"], w=["ktm"])
            for (bank, h0) in ((2, 0), (3, 4)):
              with atomic():
                for kc in range(2):
                    A("pe", lambda e, bank=bank, h0=h0, kc=kc: e.matmul(pb(bank), lhsT=lanT[:, 3 + kc, :], rhs=wkvb[:, kc, h0 * 128:h0 * 128 + 512], start=(kc == 0), stop=(kc == 1)),
                      r=["lanT", f"wkvb{kc}"], w=[f"B{bank}"])
                v = pb(bank).rearrange("p (h c) -> p h c", h=4)
                A("act", lambda e, v=v, h0=h0: e.activation(out=ktm[:, h0:h0 + 4, 0:64], in_=v[:, :, 0:64], func=AF.Copy), r=[f"B{bank}"], w=["ktm"])
                A("dve", lambda e, v=v, h0=h0: e.tensor_copy(out=vc[:, t, h0:h0 + 4, 0:64], in_=v[:, :, 64:128]), r=[f"B{bank}"], w=[f"vc{t}"])
            for g in range(2):
              with atomic():
                for j in range(4):
                    h = g * 4 + j
                    A("pe", lambda e, h=h, j=j: e.transpose(out=pbb(0)[0:96, j * 128:(j + 1) * 128], in_=qtm[:, h, :], identity=identb), r=["qtm", "identb"], w=["B0"])
                    A("pe", lambda e, h=h, j=j: e.transpose(out=pbb(0)[0:96, 512 + j * 128:512 + (j + 1) * 128], in_=ktm[:, h, :], identity=identb), r=["ktm", "identb"], w=["B0"])
                A("act", lambda e, g=g: e.activation(out=qTall[0:96, g * 4:g * 4 + 4, sub * 128:(sub + 1) * 128], in_=pbb(0)[0:96, 0:512].rearrange("p (a b) -> p a b", a=4), func=AF.Copy),
                  r=["B0"], w=[f"qTall{sp_}"])
                A("dve", lambda e, g=g: e.tensor_copy(out=kT[0:96, g * 4:g * 4 + 4, t * 128:(t + 1) * 128], in_=pbb(0)[0:96, 512:1024].rearrange("p (a b) -> p a b", a=4)),
                  r=["B0"], w=[f"kT{t}"])

        def attention(stile):
            sp_ = stile % 2
            mix = mix2[sp_]
            qTall = qTall2[sp_]
            npair = stile + 1
            for h in range(8):
                pend = []
                for p in range(npair):
                    sb_ = 4 + (p % 2)
                    bk = f"B{sb_}"
                    pk = p % 3
                    diag = (p == npair - 1)
                    for i in range(2):
                        kt = 2 * p + i
                        q0 = 128 if (diag and i == 1) else 0
                        A("pe", lambda e, sb_=sb_, kt=kt, q0=q0, h=h, i=i: e.matmul(pb(sb_)[:, i * SQ + q0:(i + 1) * SQ], lhsT=kT[0:96, h, kt * 128:(kt + 1) * 128], rhs=qTall[0:96, h, q0:SQ], start=True, stop=True),
                          r=[f"kT{kt}", f"qTall{sp_}"], w=[bk])
                    if not diag:
                        A("act", lambda e, sb_=sb_, pk=pk: e.activation(out=PT[pk], in_=pb(sb_), func=AF.Exp, scale=SC_ATT), r=[bk], w=[f"PT{pk}"])
                    else:
                        A("act", lambda e, sb_=sb_, pk=pk: e.activation(out=PT[pk][:, 0:SQ], in_=pb(sb_)[:, 0:SQ], func=AF.Exp, scale=SC_ATT), r=[bk], w=[f"PT{pk}"])
                        A("act", lambda e, sb_=sb_, pk=pk: e.activation(out=PT[pk][:, SQ + 128:2 * SQ], in_=pb(sb_)[:, SQ + 128:2 * SQ], func=AF.Exp, scale=SC_ATT), r=[bk], w=[f"PT{pk}"])
                        A("pool", lambda e, pk=pk: e.tensor_tensor(out=PT[pk][:, 0:128], in0=PT[pk][:, 0:128], in1=tri_b, op=ALU.mult), r=[f"PT{pk}", "tri_b"], w=[f"PT{pk}"])
                        A("pool", lambda e, pk=pk: e.tensor_tensor(out=PT[pk][:, SQ + 128:2 * SQ], in0=PT[pk][:, SQ + 128:2 * SQ], in1=tri_b, op=ALU.mult), r=[f"PT{pk}", "tri_b"], w=[f"PT{pk}"])
                    while len(pend) > 2:
                        pend.pop(0)()
                    for i in range(2):
                        kt = 2 * p + i
                        q0 = 128 if (diag and i == 1) else 0
                        pend.append(lambda kt=kt, pk=pk, q0=q0, h=h, i=i: A("pe", lambda e: e.matmul(pb(6)[0:65, q0:SQ], lhsT=vc[:, kt, h, :], rhs=PT[pk][:, i * SQ + q0:(i + 1) * SQ], start=(kt == 0), stop=(kt == 2 * npair - 1)),
                                                                        r=[f"vc{kt}", f"PT{pk}"], w=["B6"]))
                for f_ in pend:
                    f_()
                A("dve", lambda e: e.tensor_copy(out=oT[0:65, :], in_=pb(6)[0:65, 0:SQ]), r=["B6"], w=["oT"])
                for sub in range(QT):
                    A("pe", lambda e, sub=sub: e.transpose(out=pb(6)[:, 256 + sub * 65:256 + sub * 65 + 65], in_=oT[0:65, sub * 128:(sub + 1) * 128], identity=identf[0:65, 0:65]),
                      r=["oT", "cf"], w=["B6"])
                for sub in range(QT):
                    o = pb(6)[:, 256 + sub * 65:256 + sub * 65 + 65]
                    A("dve", lambda e, o=o: e.reciprocal(out=rd, in_=o[:, 64:65]), r=["B6"], w=["rd"])
                    A("dve", lambda e, o=o, sub=sub, h=h: e.tensor_scalar(out=mix[:, sub, 512 + h * 64:512 + (h + 1) * 64], in0=o[:, 0:64], scalar1=rd, scalar2=None, op0=ALU.mult),
                      r=["B6"], sr=["rd"], w=[f"mix{sp_}_{sub}"])

        def outproj(stile):
            sp_ = stile % 2
            mix = mix2[sp_]
            for sub in range(QT):
                t = stile * QT + sub
                dma(mix_d[t * 128:(t + 1) * 128, :], mix[:, sub, :], r=[f"mix{sp_}_{sub}"], w=[f"mixd{t}"])
                if dbg:
                    dma(dbg_d[t * 128:(t + 1) * 128, :], mix[:, sub, :], r=[f"mix{sp_}_{sub}"], w=[f"dbgd{t}"])

        def merge_n(lists):
            out = []
            for l in lists:
                out = merge(out, l) if out else list(l)
            return out

        def attn_store(stile):
            attention(stile)
            outproj(stile)

        ntile = nst * QT
        att = {}
        for j in range(ntile + 5):
            streams = []
            if j < ntile:
                streams.append(record(P_, j))
            if 0 <= j - 1 < ntile:
                streams.append(record(G_, j - 1))
                streams.append(record(L_, j - 1))
            if 0 <= j - 2 < ntile:
                streams.append(record(H_, j - 2, 0))
                streams.append(record(H_, j - 2, 1))
            if j >= 3 and (j - 3) % 2 == 0 and (j - 3) // 2 < nst:
                st_ = (j - 3) // 2
                full = record(attn_store, st_)
                half = len(full) // 2
                att[j] = full[:half]
                att[j + 1] = full[half:]
            if j in att:
                streams.append(att.pop(j))
            streams = [x for x in streams if x]
            if streams:
                feed(merge_n(streams))
        P.barrier()

        lw_rr = [0]

        def LW(dst, src_d, nk, ncols, gcol0, key):
            npiece = (ncols + 1407) // 1408
            pc = ncols // npiece
            for kc in range(nk):
                for hf in range(npiece):
                    i = lw_rr[0] % 2
                    lw_rr[0] += 1
                    sbuf = stg2[i][:, 0:pc]
                    sk = f"stgb{i}"
                    dma(sbuf, src_d[kc * 128:(kc + 1) * 128, hf * pc:(hf + 1) * pc], w=[sk])
                    sc = None if gcol0 is None or (gcol0 == 13 and kc >= 4) else gains[:, gcol0 + kc:gcol0 + kc + 1]
                    cast(dst[:, kc, hf * pc:(hf + 1) * pc], sbuf, sc, r=[sk], w=[f"{key}{kc}"])

        def LW_out():
            LW(wout, wout_d, 8, D, 13, "wout")
            dma(fnw, fnw_d.partition_broadcast(128), w=["fnw"])

        def LW_gu():
            LW(wg, wg_d, 8, DFF, 17, "wg")
            LW(wu, wu_d, 8, DFF, 17, "wu")

        def LW_d():
            LW(wd, wd_d, 22, D, None, "wd")

        def S1_loads(g, sub, x=True, mix=True):
            t = g * 4 + sub
            par = sub % 2
            if x:
                dma(h1t[:, sub, :], x_d[t * 128:(t + 1) * 128, :], w=[f"h1t{sub}"])
            if mix:
                dma(hn2[par], mix_d[t * 128:(t + 1) * 128, :], r=[f"mixd{t}"], w=[f"hn{par}"])

        def S1p(g, par):
            tb, hb, ob0, hk_ = (1, 0, 4, "hn0") if par == 0 else (2, 3, 6, "hn1")
            hnb = hn2[par]
            mTb = mixT2[par]
            for sub in (par, par + 2):
                t = g * 4 + sub
                hk = f"h1t{sub}"
                if g == 0:
                    S1_loads(g, sub)
                elif sub >= 2:
                    S1_loads(g, sub, x=False)
                for kc in range(8):
                    A("pe", lambda e, kc=kc: e.transpose(out=pbb(tb)[:, kc * 128:(kc + 1) * 128], in_=hnb[:, kc * 128:(kc + 1) * 128], identity=identb), r=[hk_, "identb"], w=[f"B{tb}"])
                A("act", lambda e: e.activation(out=mTb.rearrange("p a b -> p (a b)"), in_=pbb(tb), func=AF.Copy), r=[f"B{tb}"], w=[f"mixT{par}"])
                for half in range(2):
                    bank = ob0 + half
                    for kc in range(8):
                        A("pe", lambda e, kc=kc, half=half, bank=bank: e.matmul(pb(bank), lhsT=mTb[:, kc, :], rhs=wout[:, kc, half * 512:(half + 1) * 512], start=(kc == 0), stop=(kc == 7)),
                          r=[f"mixT{par}", f"wout{kc}"], w=[f"B{bank}"])
                    A("dve", lambda e, half=half, bank=bank, sub=sub: e.tensor_tensor(out=h1t[:, sub, half * 512:(half + 1) * 512], in0=pb(bank), in1=h1t[:, sub, half * 512:(half + 1) * 512], op=ALU.add),
                      r=[f"B{bank}", hk], w=[hk])
                A("act", lambda e, sub=sub: e.activation(out=hnb, in_=h1t[:, sub, :], func=AF.Square, scale=float(D ** -0.5), accum_out=ssq[:, par:par + 1]), r=[hk], w=[hk_, f"ssq_{par}"])
                A("act", lambda e: e.activation(out=rstd2[:, par:par + 1], in_=ssq[:, par:par + 1], func=AF.Ln, bias=epst), r=[f"ssq_{par}"], sr=["epst"], w=[f"rstdl_{par}"])
                A("act", lambda e: e.activation(out=rstd[:, par:par + 1], in_=rstd2[:, par:par + 1], func=AF.Exp, scale=-0.5), r=[f"rstdl_{par}"], w=[f"rstd_{par}"])
                A("dve", lambda e, sub=sub: e.tensor_scalar(out=hnb, in0=h1t[:, sub, :], scalar1=rstd[:, par:par + 1], scalar2=None, op0=ALU.mult), r=[hk], sr=[f"rstd_{par}"], w=[hk_])
                for kc in range(8):
                    A("pe", lambda e, kc=kc: e.transpose(out=pbb(hb)[:, kc * 128:(kc + 1) * 128], in_=hnb[:, kc * 128:(kc + 1) * 128], identity=identb), r=[hk_, "identb"], w=[f"B{hb}"])
                A("act", lambda e, sub=sub: e.activation(out=hnT[:, :, sub * 128:(sub + 1) * 128], in_=pbb(hb).rearrange("p (a b) -> p a b", a=8), func=AF.Copy), r=[f"B{hb}"], w=[f"hnT{sub}"])

        def S2_(g):
            for fc in range(22):
                gb = 2 + 2 * (fc % 2)
                ub = gb + 1
                fs = slice(fc * 128, (fc + 1) * 128)
                for kc in range(8):
                    A("pe", lambda e, kc=kc, fs=fs, gb=gb: e.matmul(pb(gb), lhsT=wg[:, kc, fs], rhs=hnT[:, kc, :], start=(kc == 0), stop=(kc == 7)), r=[f"wg{kc}", "hnT0", "hnT1", "hnT2", "hnT3"], w=[f"B{gb}"])
                for kc in range(8):
                    A("pe", lambda e, kc=kc, fs=fs, ub=ub: e.matmul(pb(ub), lhsT=wu[:, kc, fs], rhs=hnT[:, kc, :], start=(kc == 0), stop=(kc == 7)), r=[f"wu{kc}", "hnT0", "hnT1", "hnT2", "hnT3"], w=[f"B{ub}"])
                sgb = sg[fc % 2]
                A("act", lambda e, gb=gb, sgb=sgb: e.activation(out=sgb, in_=pb(gb), func=AF.Silu), r=[f"B{gb}"], w=["sg"])
                A("dve", lambda e, ub=ub, sgb=sgb, fc=fc: e.tensor_tensor(out=actT[:, fc, :], in0=pb(ub), in1=sgb, op=ALU.mult), r=[f"B{ub}", "sg"], w=[f"actT{fc}"])

        def S3_(g):
            for sub in range(4):
                t = g * 4 + sub
                for half in range(2):
                    bank = 6 + half
                    for fc in range(22):
                        A("pe", lambda e, fc=fc, sub=sub, half=half, bank=bank: e.matmul(pb(bank), lhsT=actT[:, fc, sub * 128:(sub + 1) * 128], rhs=wd[:, fc, half * 512:(half + 1) * 512], start=(fc == 0), stop=(fc == 21)),
                          r=[f"actT{fc}", f"wd{fc}"], w=[f"B{bank}"])
                    A("dve", lambda e, sub=sub, half=half, bank=bank: e.tensor_tensor(out=h1t[:, sub, half * 512:(half + 1) * 512], in0=pb(bank), in1=h1t[:, sub, half * 512:(half + 1) * 512], op=ALU.add),
                      r=[f"B{bank}", f"h1t{sub}"], w=[f"h1t{sub}"])
                A("act", lambda e, sub=sub: e.activation(out=mixT2[1].rearrange("p a b -> p (a b)"), in_=h1t[:, sub, :], func=AF.Square, scale=float(D ** -0.5), accum_out=ssq[:, 3:4]), r=[f"h1t{sub}"], w=["mixT1", "ssq3"])
                A("act", lambda e: e.activation(out=rstd2[:, 3:4], in_=ssq[:, 3:4], func=AF.Ln, bias=epst), r=["ssq3", "epst"], w=["rstd3l"])
                A("act", lambda e: e.activation(out=rstd[:, 3:4], in_=rstd2[:, 3:4], func=AF.Exp, scale=-0.5), r=["rstd3l"], w=["rstd3"])
                A("dve", lambda e, sub=sub: e.scalar_tensor_tensor(out=h1t[:, sub, :], in0=h1t[:, sub, :], scalar=rstd[:, 3:4], in1=fnw, op0=ALU.mult, op1=ALU.mult),
                  r=[f"h1t{sub}", "fnw"], sr=["rstd3"], w=[f"h1t{sub}"])
                dma(out_d[t * 128:(t + 1) * 128, :], h1t[:, sub, :], r=[f"h1t{sub}"], w=[f"outd{t}"])
                if g + 1 < S // 512:
                    S1_loads(g + 1, sub, mix=(sub < 2))


        if do_p2:
            feed(record(LW_out))
            for g in range(S // 512):
                s1 = merge(record(S1p, g, 0), record(S1p, g, 1))
                if g == 0:
                    feed(merge(s1, record(LW_gu)))
                    feed(merge(record(S2_, g), record(LW_d)))
                else:
                    feed(s1)
                    feed(record(S2_, g))
                feed(record(S3_, g))

        P.emit(nc, st)
    return nc


_CONSTS = None


def _consts():
    global _CONSTS
    if _CONSTS is None:
        identb = np.eye(128, dtype=np.float32).astype(ml_dtypes.bfloat16)
        identf = np.eye(128, dtype=np.float32)
        tri = np.triu(np.ones((128, 128), dtype=np.float32))
        ones = np.ones((128, 128), dtype=np.float32)
        _CONSTS = (identb, np.ascontiguousarray(np.concatenate([identf, tri, ones], axis=1)))
    return _CONSTS


def kernel(x, positions, attn_norm_w, w_in, b_gates, mlstm_norm_w, q_a_norm_w, w_q_b,
           kv_a_norm_w, w_kv_b, w_out, ffn_norm_w, w_gate, w_up, w_down, final_norm_w):
    f = lambda a: np.ascontiguousarray(np.asarray(a, dtype=np.float32))
    x = f(x)
    positions = np.asarray(positions, dtype=np.int32)
    identb, cf = _consts()
    col = lambda v, n: f(v).reshape(n, 128).T
    gains = np.ascontiguousarray(np.concatenate([
        col(attn_norm_w[0], 8), col(q_a_norm_w[0], 3), col(kv_a_norm_w[0], 2),
        col(mlstm_norm_w[0].reshape(-1), 4), col(ffn_norm_w[0], 8)], axis=1))
    shared = {
        "w_in": f(w_in[0]), "w_q_b": f(w_q_b[0]), "w_kv_b": f(w_kv_b[0]), "w_out": f(w_out[0]),
        "w_gate": f(w_gate[0]), "w_up": f(w_up[0]), "w_down": f(w_down[0]),
        "gains": gains, "final_nw": f(final_norm_w).reshape(1, D), "b_gates": f(b_gates).reshape(1, 8),
        "identb": identb, "cf": cf,
    }
    nc = build_program()
    in_maps = []
    for b in range(8):
        m = dict(shared)
        m["x"] = x[b]
        m["pos"] = np.ascontiguousarray(positions[b].reshape(NT, 128).T)
        in_maps.append(m)
    res = run_bass_kernel_spmd(nc, in_maps, core_ids=list(range(8)))
    return np.stack([np.asarray(r["out"], dtype=np.float32) for r in res.results], axis=0)
```

```python
import math
import numpy as np
import ml_dtypes
from contextlib import ExitStack
import concourse.bass as bass
import concourse.mybir as mybir
from concourse.bass_utils import run_bass_kernel_spmd

F32 = mybir.dt.float32
BF16 = mybir.dt.bfloat16
I32 = mybir.dt.int32
U8 = mybir.dt.uint8
AF = mybir.ActivationFunctionType
ALU = mybir.AluOpType
AX = mybir.AxisListType

S = 4096
D = 1024
DIN = 2728
DFF = 2816
NT = S // 128
QT = 2
NST = NT // QT
SQ = QT * 128
EPS = 1e-6
GEN = 20000


class Op:
    __slots__ = ("eng", "fn", "deps", "is_dma", "sig", "sem", "val", "prev", "eidx")

    def __init__(self, eng, fn, is_dma):
        self.eng = eng
        self.fn = fn
        self.is_dma = is_dma
        self.deps = []
        self.sig = False
        self.sem = None
        self.val = 0
        self.prev = None


class Prog:
    NDMA = 12
    WINDOW = 16
    STRICT = True

    def __init__(self):
        self.ops = []
        self.last_w = {}
        self.readers = {}
        self.last_eng = {}
        self.dma_recent = {}
        self.pending_barrier = {}
        self.ecount = {}
        self.t_last_w = {}
        self.t_readers = {}

    def add(self, eng, fn, r=(), w=(), dma=False, sr=()):
        op = Op(eng, fn, dma)
        deps = []
        seen = set()
        isb = lambda k: len(k) >= 2 and k[0] == "B" and k[1].isdigit()
        nb = lambda k: k[:2] if isb(k) else k
        tr = list(dict.fromkeys([nb(k) for k in list(r) + list(sr)]))
        tw = list(dict.fromkeys([nb(k) for k in w]))
        w = list(dict.fromkeys([nb(k) for k in list(w) + [k for k in r if isb(k)]]))
        r = [k for k in r if not isb(k)] + list(sr)
        eidx = self.ecount.get(eng, 0)
        self.ecount[eng] = eidx + 1
        op.eidx = eidx

        def push(d, raw):
            if id(d) in seen:
                return
            if d.is_dma or d.eng != eng or dma:
                seen.add(id(d))
                deps.append(d)
            elif raw and eng != "pe" and eidx - d.eidx <= self.WINDOW:
                seen.add(id(d))
                deps.append(d)

        for k in r:
            d = self.last_w.get(k)
            if d is not None:
                push(d, True)
        for k in w:
            d = self.last_w.get(k)
            if d is not None:
                push(d, isb(k) and k in tr)
            for d in self.readers.get(k, ()):
                push(d, False)
        if self.STRICT:
            def spush(d):
                if id(d) not in seen and d.eng == eng and not d.is_dma and not (eng == "pe"):
                    seen.add(id(d))
                    deps.append(d)
            for k in tr:
                d = self.t_last_w.get(k)
                if d is not None:
                    spush(d)
            for k in tw:
                d = self.t_last_w.get(k)
                if d is not None:
                    spush(d)
                for d in self.t_readers.get(k, ()):
                    spush(d)
        for k in tw:
            self.t_last_w[k] = op
            self.t_readers[k] = []
        for k in tr:
            self.t_readers.setdefault(k, []).append(op)
        if eng in self.pending_barrier:
            for d in self.pending_barrier.pop(eng):
                push(d, False)
        for k in w:
            self.last_w[k] = op
            self.readers[k] = []
        for k in r:
            self.readers.setdefault(k, []).append(op)
        op.deps = deps
        for d in deps:
            d.sig = True
        self.ops.append(op)
        self.last_eng[eng] = op
        if dma:
            lst = self.dma_recent.setdefault(eng, [])
            lst.append(op)
            if len(lst) > self.NDMA:
                lst.pop(0)
        return op

    def barrier(self):
        deps = [o for o in self.last_eng.values() if not o.is_dma]
        for lst in self.dma_recent.values():
            deps.extend(lst)
        for e in ("pe", "act", "dve", "pool", "sp"):
            self.pending_barrier[e] = list(deps)

    def emit(self, nc, stack):
        names = {"pe": "tensor", "act": "scalar", "dve": "vector", "pool": "gpsimd", "sp": "sync"}
        per = {e: [o for o in self.ops if o.eng == e] for e in names}
        sems = {}

        def getsem(name):
            if name not in sems:
                sems[name] = stack.enter_context(nc.semaphore(name))
            return sems[name]

        for e, lst in per.items():
            cnt = 0
            dcnt = 0
            slot_last = {}
            for o in lst:
                if o.is_dma:
                    slot = dcnt % self.NDMA
                    o.sem = getsem(f"d_{e}_{slot}")
                    o.prev = slot_last.get(slot)
                    o.val = (o.prev.val if o.prev is not None else 0) + 16
                    slot_last[slot] = o
                    dcnt += 1
                elif o.sig:
                    o.sem = getsem(f"c_{e}_{cnt // GEN}")
                    o.val = cnt % GEN + 1
                    cnt += 1
        block = stack.enter_context(nc.Block())

        def run(e, lst):
            def body(eng):
                waited = {}
                for o in lst:
                    ws = [(d.sem, d.val) for d in o.deps]
                    if o.is_dma and o.prev is not None:
                        ws.append((o.prev.sem, o.prev.val))
                    for s, v in ws:
                        if waited.get(s.name, 0) < v:
                            eng.wait_ge(s, v)
                            waited[s.name] = v
                    ins = o.fn(eng)
                    if o.is_dma:
                        ins.then_inc(o.sem, 16)
                    elif o.sig:
                        ins.then_inc(o.sem, 1)
                last = {}
                for o in lst:
                    if o.is_dma:
                        last[o.sem.name] = (o.sem, o.val)
                for s, v in last.values():
                    if waited.get(s.name, 0) < v:
                        eng.wait_ge(s, v)
            return body

        for e, lst in per.items():
            if lst:
                getattr(block, names[e])(run(e, lst))


def build_program(nst=NST, do_p2=True, dbg=False, lvl=99):
    nc = bass.Bass("TRN2", target_bir_lowering=False, dynamic_dma_scratch_size=256)
    dt = lambda n, s, d, k="ExternalInput": nc.dram_tensor(n, s, d, kind=k).ap()
    x_d = dt("x", [S, D], F32)
    pos_d = dt("pos", [128, NT], I32)
    w_in_d = dt("w_in", [D, DIN], F32)
    wqb_d = dt("w_q_b", [384, 768], F32)
    wkvb_d = dt("w_kv_b", [256, 1024], F32)
    wout_d = dt("w_out", [D, D], F32)
    wg_d = dt("w_gate", [D, DFF], F32)
    wu_d = dt("w_up", [D, DFF], F32)
    wd_d = dt("w_down", [DFF, D], F32)
    gains_d = dt("gains", [128, 25], F32)
    fnw_d = dt("final_nw", [1, D], F32)
    bg_d = dt("b_gates", [1, 8], F32)
    identb_d = dt("identb", [128, 128], BF16)
    cf_d = dt("cf", [128, 384], F32)
    out_d = dt("out", [S, D], F32, "ExternalOutput")
    mix_d = dt("mixs", [S, D], BF16, "Internal")
    dbg_d = dt("dbg", [S, D], BF16, "ExternalOutput") if dbg else None
    dbgf_d = dt("dbgf", [128, 1024], F32, "ExternalOutput") if dbg else None

    P = Prog()
    inv_freq = (np.float32(10000.0) ** (-np.arange(16, dtype=np.float32) / np.float32(16))).astype(np.float32)

    with ExitStack() as st:
        ARENA = 222 * 1024
        arena = st.enter_context(nc.sbuf_tensor("arena", [128, ARENA], U8))
        banks = [st.enter_context(nc.psum_tensor(f"bank{i}", [128, 512], F32)) for i in range(8)]
        off = [0]

        def alloc(shape, dtype, at=None):
            nb = int(np.prod(shape[1:])) * mybir.dt.size(dtype)
            nb = (nb + 63) // 64 * 64
            o = off[0] if at is None else at
            if at is None:
                off[0] += nb
            assert o + nb <= ARENA, (o, nb)
            ap = arena[:, o:o + nb].bitcast(dtype)
            n = int(np.prod(shape[1:]))
            ap = ap[:, 0:n]
            if len(shape) == 3:
                ap = ap.rearrange("p (a b) -> p a b", a=shape[1])
            elif len(shape) == 4:
                ap = ap.rearrange("p (a b c) -> p a b c", a=shape[1], b=shape[2])
            return ap

        def pb(i):
            return banks[i][:]

        def pbb(i):
            return banks[i][:].bitcast(BF16)

        identb = alloc([128, 128], BF16)
        cf = alloc([128, 384], F32)
        identf = cf[:, 0:128]
        tri = cf[:, 128:256]
        ones = cf[:, 256:384]
        gains = alloc([128, 25], F32)
        epst = alloc([128, 1], F32)
        onec = alloc([128, 1], F32)
        ssq = alloc([128, 4], F32)
        rstd = alloc([128, 4], F32)
        rstd2 = alloc([128, 4], F32)
        common_end = off[0]
        junk_at = off[0]
        junk = alloc([128, 1024], BF16)

        kT_at = off[0]
        kT = alloc([128, 8, S], BF16)
        vc_at = off[0]
        vc = alloc([128, NT, 8, 65], BF16)
        w_in = alloc([128, 8, DIN], BF16)
        wqb = alloc([128, 3, 768], BF16)
        wkvb = alloc([128, 2, 1024], BF16)
        cosT = alloc([128, NT, 16], F32)
        sinT = alloc([128, NT, 16], F32)
        bgate = alloc([128, 8], F32)
        tri_b = alloc([128, 128], BF16)
        Cst = alloc([128, 4, 129], F32)
        mprev = [alloc([128, 4], F32) for _ in range(2)]
        vext = [alloc([128, 4, 129], BF16) for _ in range(3)]
        p1_misc = off[0]
        stage_at = off[0]
        xring = [alloc([128, D], F32) for _ in range(2)]
        u_t = alloc([128, D], BF16)
        uT = alloc([128, 8, 128], BF16)
        q_tm = [alloc([128, 512], BF16) for _ in range(2)]
        qT4 = [alloc([128, 4, 128], BF16) for _ in range(3)]
        k_tm = [alloc([128, 512], BF16) for _ in range(3)]
        og = [alloc([128, 512], BF16) for _ in range(3)]
        ogt = alloc([128, 512], F32, at=junk_at)
        lat = [alloc([128, 680], F32) for _ in range(2)]
        gts = [alloc([128, 8], F32) for _ in range(2)]
        ef = [alloc([128, 4], F32) for _ in range(2)]
        spl = [alloc([128, 12], F32) for _ in range(2)]
        apr = [alloc([128, 4], F32) for _ in range(2)]
        cst_ = [alloc([128, 4], F32) for _ in range(2)]
        Mx = [alloc([128, 4], F32) for _ in range(2)]
        args = [alloc([128, 12], F32) for _ in range(2)]
        exs = [alloc([128, 12], F32) for _ in range(2)]
        ke = [alloc([128, 128], BF16) for _ in range(2)]
        keT = [alloc([128, 128], BF16) for _ in range(2)]
        PTm = [alloc([128, 128], BF16) for _ in range(2)]
        Cb = [alloc([128, 129], BF16) for _ in range(2)]
        sm = [alloc([128, 8], F32) for _ in range(2)]
        junk2 = alloc([128, 128], BF16)
        junk3 = alloc([128, 384], BF16)
        rk = [alloc([128, 16], F32) for _ in range(4)]
        lan = alloc([128, 640], BF16)
        lanT = alloc([128, 5, 128], BF16)
        qtm = alloc([128, 8, 96], BF16)
        ktm = alloc([128, 8, 96], BF16)
        rt = [alloc([128, 8, 16], F32) for _ in range(4)]
        kro = alloc([128, 32], BF16)
        qTall2 = [alloc([128, 8, SQ], BF16) for _ in range(2)]
        PT = [alloc([128, 2 * SQ], BF16) for _ in range(3)]
        oT = alloc([128, SQ], F32)
        mix2 = [alloc([128, QT, D], BF16) for _ in range(2)]
        rd = alloc([128, 1], F32)
        p1_end = off[0]
        VT0 = 10
        stg = [alloc([128, DFF], F32, at=vc_at + VT0 * 1040 + i * 11264) for i in range(2)]
        assert VT0 * 1040 + 2 * 11264 <= NT * 1040
        posi = alloc([128, NT], I32, at=kT_at)
        posf = alloc([128, NT], F32, at=kT_at + 128)
        ang = alloc([128, NT, 32], F32, at=kT_at + 256)
        ki = alloc([128, NT * 32], I32, at=kT_at + 256 + 4096)
        kf = alloc([128, NT * 32], F32, at=kT_at + 256 + 8192)

        off[0] = common_end
        wg = alloc([128, 8, DFF], BF16)
        wu = alloc([128, 8, DFF], BF16)
        wd = alloc([128, 22, D], BF16)
        wout = alloc([128, 8, D], BF16)
        fnw = alloc([128, D], F32)
        h1t = alloc([128, 4, D], F32)
        hn2 = [alloc([128, D], BF16) for _ in range(2)]
        hn = hn2[0]
        mixT2 = [alloc([128, 8, 128], BF16) for _ in range(2)]
        hnT = alloc([128, 8, 512], BF16)
        sg_ = alloc([128, 512], BF16)
        sg = [sg_, sg_]
        act_at = off[0]
        actT = alloc([128, 22, 512], BF16)
        stg2 = [alloc([128, 1408], F32) for i in range(2)]
        p2_end = off[0]

        cur = [None]
        atom = [0]

        def A(eng, fn, r=(), w=(), dma=False, sr=()):
            if cur[0] is not None:
                op = (eng, fn, tuple(r), tuple(w), dma, tuple(sr))
                if atom[0] and cur[0] and cur[0][-1][0] == "open":
                    cur[0][-1][1].append(op)
                elif atom[0]:
                    cur[0].append(["open", [op]])
                else:
                    cur[0].append(["seg", [op]])
            else:
                P.add(eng, fn, r=r, w=w, dma=dma, sr=sr)

        class atomic:
            def __enter__(self):
                atom[0] += 1

            def __exit__(self, *a_):
                atom[0] -= 1
                if atom[0] == 0 and cur[0] and cur[0][-1][0] == "open":
                    cur[0][-1][0] = "seg"

        def record(f, *args):
            cur[0] = []
            f(*args)
            lst = [seg[1] for seg in cur[0]]
            cur[0] = None
            return lst

        def feed(segs):
            for seg in segs:
                for o in seg:
                    P.add(o[0], o[1], r=o[2], w=o[3], dma=o[4], sr=o[5])

        def merge(l1, l2):
            n1 = sum(len(x) for x in l1)
            n2 = sum(len(x) for x in l2)
            out = []
            i = j = 0
            c1 = c2 = 0
            while i < len(l1) or j < len(l2):
                if j >= len(l2) or (i < len(l1) and (c1 + 0.5 * len(l1[i])) * n2 <= (c2 + 0.5 * len(l2[j])) * n1):
                    out.append(l1[i]); c1 += len(l1[i]); i += 1
                else:
                    out.append(l2[j]); c2 += len(l2[j]); j += 1
            return out

        cast_rr = [0]

        def cast(out, in_, scale, r, w):
            i = cast_rr[0]
            cast_rr[0] += 1
            rr = list(r) + (["gains"] if scale is not None else [])
            if i % 2 == 0:
                if scale is None:
                    A("act", lambda e: e.activation(out=out, in_=in_, func=AF.Copy), r=rr, w=w)
                else:
                    A("act", lambda e: e.activation(out=out, in_=in_, func=AF.Identity, scale=scale), r=rr, w=w)
            else:
                if scale is None:
                    A("dve", lambda e: e.tensor_copy(out=out, in_=in_), r=rr, w=w)
                else:
                    A("dve", lambda e: e.tensor_scalar(out=out, in0=in_, scalar1=scale, scalar2=None, op0=ALU.mult), r=rr, w=w)

        def dma(out, in_, r=(), w=()):
            A("sp", lambda e: e.dma_start(out=out, in_=in_), r=r, w=w, dma=True)

        def dump(ap, c0, n, rk):
            if dbg:
                npart = ap.shape[0]
                A("sp", lambda e: e.dma_start(out=dbgf_d[0:npart, c0:c0 + n], in_=ap, allow_slow_non_contiguous=True), r=rk, w=[f"dbgf{c0}"], dma=True)

        def evac(i, out, in_, r, w):
            if i % 2 == 0:
                A("act", lambda e: e.activation(out=out, in_=in_, func=AF.Copy), r=r, w=w)
            else:
                A("dve", lambda e: e.tensor_copy(out=out, in_=in_), r=r, w=w)

        dma(identb, identb_d[:, :], w=["identb"])
        dma(cf, cf_d[:, :], w=["cf"])
        dma(gains, gains_d[:, :], w=["gains"])
        dma(bgate, bg_d.partition_broadcast(128), w=["bgate"])
        dma(posi, pos_d[:, :], w=["posi"])
        A("dve", lambda e: e.memset(epst, EPS), w=["epst"])
        A("dve", lambda e: e.memset(onec, 1.0), w=["onec"])
        A("dve", lambda e: e.tensor_copy(out=tri_b, in_=tri), r=["cf"], w=["tri_b"])
        A("dve", lambda e: e.memset(Cst.rearrange("p a b -> p (a b)"), 0.0), w=["C0", "C1", "C2", "C3"])
        A("dve", lambda e: e.memset(mprev[0], 0.0), w=["mprev0"])
        A("pool", lambda e: e.memset(vc[:, 0:VT0].rearrange("p a b c -> p (a b c)"), 1.0), w=[f"vc{t}" for t in range(VT0)])
        for b in range(3):
            A("pool", lambda e, b=b: e.memset(vext[b].rearrange("p a b -> p (a b)"), 1.0), w=[f"vext{b}"])

        A("dve", lambda e: e.tensor_copy(out=posf, in_=posi), r=["posi"], w=["posf"])
        for j in range(16):
            A("dve", lambda e, j=j: e.tensor_scalar(out=ang[:, :, j], in0=posf, scalar1=float(inv_freq[j]), scalar2=None, op0=ALU.mult),
              r=["posf"], w=["ang"])
        A("dve", lambda e: e.tensor_scalar(out=ang[:, :, 16:32], in0=ang[:, :, 0:16], scalar1=float(np.pi / 2), scalar2=None, op0=ALU.add),
          r=["ang"], w=["ang"])
        angf = ang.rearrange("p a b -> p (a b)")
        TWO_PI = float(2 * np.pi)
        A("dve", lambda e: e.tensor_scalar(out=ki, in0=angf, scalar1=float(1.0 / TWO_PI), scalar2=None, op0=ALU.mult), r=["ang"], w=["ki"])
        A("dve", lambda e: e.tensor_copy(out=kf, in_=ki), r=["ki"], w=["kf"])
        A("dve", lambda e: e.scalar_tensor_tensor(out=angf, in0=kf, scalar=-TWO_PI, in1=angf, op0=ALU.mult, op1=ALU.add), r=["kf", "ang"], w=["ang"])
        A("dve", lambda e: e.tensor_scalar(out=kf, in0=angf, scalar1=float(np.pi), scalar2=-TWO_PI, op0=ALU.is_gt, op1=ALU.mult), r=["ang"], w=["kf"])
        A("dve", lambda e: e.tensor_tensor(out=angf, in0=angf, in1=kf, op=ALU.add), r=["ang", "kf"], w=["ang"])
        A("dve", lambda e: e.tensor_scalar(out=kf, in0=angf, scalar1=float(-np.pi), scalar2=TWO_PI, op0=ALU.is_lt, op1=ALU.mult), r=["ang"], w=["kf"])
        A("dve", lambda e: e.tensor_tensor(out=angf, in0=angf, in1=kf, op=ALU.add), r=["ang", "kf"], w=["ang"])
        A("dve", lambda e: e.tensor_scalar(out=angf, in0=angf, scalar1=3.14159, scalar2=-3.14159, op0=ALU.min, op1=ALU.max), r=["ang"], w=["ang"])
        A("act", lambda e: e.activation(out=sinT, in_=ang[:, :, 0:16], func=AF.Sin), r=["ang"], w=["sinT"])
        A("act", lambda e: e.activation(out=cosT, in_=ang[:, :, 16:32], func=AF.Sin), r=["ang"], w=["cosT"])

        def load_w(dst, src_d, nk, ncols, gcol0, key, stgs, skey):
            for kc in range(nk):
                s = stgs[kc % 2][:, 0:ncols]
                sk = f"{skey}{kc % 2}"
                dma(s, src_d[kc * 128:(kc + 1) * 128, :], w=[sk])
                sc = None if gcol0 is None or (gcol0 == 13 and kc >= 4) else gains[:, gcol0 + kc:gcol0 + kc + 1]
                cast(dst[:, kc, :], s, sc, r=[sk], w=[f"{key}{kc}"])

        load_w(w_in, w_in_d, 8, DIN, 0, "w_in", stg, "stg")
        load_w(wqb, wqb_d, 3, 768, 8, "wqb", stg, "stg")
        load_w(wkvb, wkvb_d, 2, 1024, 11, "wkvb", stg, "stg")
        A("pool", lambda e: e.memset(vc[:, VT0:NT].rearrange("p a b c -> p (a b c)"), 1.0), w=["stg0", "stg1"] + [f"vc{t}" for t in range(VT0, NT)])

        WIN = [f"w_in{k}" for k in range(8)]
        SC_ATT = float(96.0 ** -0.5)

        def Xload(t):
            dma(xring[t % 2], x_d[t * 128:(t + 1) * 128, :], w=[f"x{t % 2}"])

        def P_(t):
            tp = t % 2
            t3 = t % 3
            xt = xring[tp]
            xk = f"x{tp}"
            if t == 0:
                Xload(0)
            A("act", lambda e: e.activation(out=junk, in_=xt, func=AF.Square, scale=float(D ** -0.5), accum_out=ssq[:, 0:1]),
              r=[xk], w=["junk", "ssq"])
            A("act", lambda e: e.activation(out=rstd[:, 3:4], in_=ssq[:, 0:1], func=AF.Ln, bias=epst), r=["ssq"], sr=["epst"], w=["rstdl"])
            A("act", lambda e: e.activation(out=rstd[:, 0:1], in_=rstd[:, 3:4], func=AF.Exp, scale=-0.5), r=["rstdl"], w=["rstd"])
            A("dve", lambda e: e.tensor_scalar(out=u_t, in0=xt, scalar1=rstd[:, 0:1], scalar2=None, op0=ALU.mult), r=[xk], sr=["rstd"], w=["u"])
            with atomic():
                for kc in range(8):
                    A("pe", lambda e, kc=kc: e.transpose(out=pbb(0)[:, kc * 128:(kc + 1) * 128], in_=u_t[:, kc * 128:(kc + 1) * 128], identity=identb),
                      r=["u", "identb"], w=["B0"])
                A("dve", lambda e: e.tensor_copy(out=uT.rearrange("p a b -> p (a b)"), in_=pbb(0)), r=["B0"], w=["uT"])

            def proj(bank, c0, cn):
                for kc in range(8):
                    A("pe", lambda e, kc=kc: e.matmul(pb(bank)[:, 0:cn], lhsT=uT[:, kc, :], rhs=w_in[:, kc, c0:c0 + cn], start=(kc == 0), stop=(kc == 7)),
                      r=["uT", WIN[kc]], w=[f"B{bank}"])

            vb = vext[t3]
            with atomic():
                proj(2, 0, 512)
                A("act", lambda e: e.activation(out=q_tm[tp], in_=pb(2), func=AF.Copy, scale=float(128.0 ** -0.5)), r=["B2"], w=[f"q_tm{tp}"])
            with atomic():
                proj(3, 512, 512)
                A("dve", lambda e: e.tensor_copy(out=k_tm[t3], in_=pb(3)), r=["B3"], w=[f"k_tm{t3}"])
            with atomic():
                proj(2, 1024, 512)
                A("act", lambda e: e.activation(out=vb[:, :, 0:128], in_=pb(2).rearrange("p (a b) -> p a b", a=4), func=AF.Copy), r=["B2"], w=[f"vext{t3}"])
            with atomic():
                proj(3, 1536, 512)
                A("act", lambda e: e.activation(out=ogt, in_=pb(3), func=AF.Exp, scale=-1.0), r=["B3"], w=["junk"])
            A("dve", lambda e: e.tensor_scalar(out=ogt, in0=ogt, scalar1=1.0, scalar2=None, op0=ALU.add), r=["junk"], w=["junk"])
            def _rcp(e):
                with nc.allow_low_precision("output gate is stored in bf16 (it only scales the bf16 mixer output)"):
                    return e.reciprocal(out=og[t3], in_=ogt)
            A("dve", _rcp, r=["junk"], w=[f"og{t3}"])
            with atomic():
                proj(2, 2048, 512)
                A("dve", lambda e: e.tensor_copy(out=lat[tp][:, 0:512], in_=pb(2)), r=["B2"], w=[f"lat{tp}"])
            with atomic():
                proj(3, 2560, 168)
                A("dve", lambda e: e.tensor_copy(out=lat[tp][:, 512:680], in_=pb(3)[:, 0:168]), r=["B3"], w=[f"lat{tp}"])
            with atomic():
                for h in range(4):
                    A("pe", lambda e, h=h: e.transpose(out=pbb(0)[:, h * 128:(h + 1) * 128], in_=q_tm[tp][:, h * 128:(h + 1) * 128], identity=identb),
                      r=[f"q_tm{tp}", "identb"], w=["B0"])
                A("dve", lambda e: e.tensor_copy(out=qT4[t3].rearrange("p a b -> p (a b)"), in_=pbb(0)[:, 0:512]), r=["B0"], w=[f"qT4{t3}"])

        def G_(t):
            tp = t % 2
            latt = lat[tp]
            g_, sp, ap_, mx, ar, ex, cc = gts[tp], spl[tp], apr[tp], Mx[tp], args[tp], exs[tp], cst_[tp]
            mp = mprev[tp]
            mn = mprev[1 - tp]
            mpk = f"mprev{tp}"
            mnk = f"mprev{1 - tp}"
            gk = f"g{tp}"
            A("dve", lambda e: e.tensor_tensor(out=g_, in0=latt[:, 0:8], in1=bgate, op=ALU.add), r=[f"lat{tp}", "bgate"], w=[gk + "gts"])
            A("act", lambda e: e.activation(out=ef[tp], in_=g_[:, 4:8], func=AF.Exp, scale=-1.0), r=[gk + "gts"], w=[gk + "ef"])
            A("act", lambda e: e.activation(out=g_[:, 4:8], in_=ef[tp], func=AF.Ln, bias=onec), r=[gk + "ef", "onec"], w=[gk + "gts"])
            with atomic():
                A("pe", lambda e: e.matmul(pb(0)[:, 0:4], lhsT=tri, rhs=g_[:, 4:8], start=True, stop=True), r=["cf", gk + "gts"], w=["B0"])
                A("pe", lambda e: e.matmul(pb(0)[:, 4:12], lhsT=ones, rhs=g_, start=True, stop=True), r=["cf", gk + "gts"], w=["B0"])
                A("dve", lambda e: e.tensor_copy(out=sp, in_=pb(0)[:, 0:12]), r=["B0"], w=[gk + "sp"])
            A("dve", lambda e: e.tensor_tensor(out=ap_, in0=sp[:, 0:4], in1=g_[:, 0:4], op=ALU.add), r=[gk + "sp", gk + "gts"], w=[gk + "apr"])
            A("dve", lambda e: e.tensor_scalar(out=cc, in0=sp[:, 8:12], scalar1=0.5, scalar2=None, op0=ALU.mult), r=[gk + "sp"], w=[gk + "cc"])
            A("dve", lambda e: e.scalar_tensor_tensor(out=cc, in0=sp[:, 4:8], scalar=float(1.0 / 128), in1=cc, op0=ALU.mult, op1=ALU.add), r=[gk + "sp", gk + "cc"], w=[gk + "cc"])
            A("dve", lambda e: e.tensor_tensor(out=mx, in0=cc, in1=mp, op=ALU.max), r=[gk + "cc", mpk], w=[gk + "Mx"])
            A("dve", lambda e: e.tensor_tensor(out=mn, in0=mx, in1=sp[:, 8:12], op=ALU.subtract), r=[gk + "sp", gk + "Mx"], w=[mnk])
            A("dve", lambda e: e.tensor_tensor(out=ar[:, 0:4], in0=ap_, in1=mx, op=ALU.subtract), r=[gk + "apr", gk + "Mx"], w=[gk + "args"])
            A("dve", lambda e: e.tensor_tensor(out=ar[:, 4:8], in0=mp, in1=mx, op=ALU.subtract), r=[mpk, gk + "Mx"], w=[gk + "args"])
            A("dve", lambda e: e.tensor_tensor(out=ar[:, 8:12], in0=sp[:, 0:4], in1=mx, op=ALU.subtract), r=[gk + "sp", gk + "Mx"], w=[gk + "args"])
            A("act", lambda e: e.activation(out=ex, in_=ar, func=AF.Exp), r=[gk + "args"], w=[gk + "exs"])

        def H_(t, par):
            tp = t % 2
            t3 = t % 3
            sp_ = (t // QT) % 2
            mix = mix2[sp_]
            sub = t % QT
            vb = vext[t3]
            vk = f"vext{t3}"
            ex = exs[tp]
            ek = f"g{tp}exs"
            for h in (par, par + 2):
                b = h % 2
                bank = 1 if b == 0 else 7
                bk = f"B{bank}"
                hs = slice(h * 128, (h + 1) * 128)
                A("dve", lambda e, h=h, b=b, hs=hs: e.tensor_scalar(out=ke[b], in0=k_tm[t3][:, hs], scalar1=ex[:, h:h + 1], scalar2=None, op0=ALU.mult),
                  r=[f"k_tm{t3}"], sr=[ek], w=[f"ke{b}"])
                A("pe", lambda e, b=b, bank=bank: e.transpose(out=pbb(bank)[:, 776:904], in_=ke[b], identity=identb), r=[f"ke{b}", "identb"], w=[bk])
                A("dve", lambda e, b=b, bank=bank: e.tensor_copy(out=keT[b], in_=pbb(bank)[:, 776:904]), r=[bk], w=[f"keT{b}"])
                A("pe", lambda e, b=b, bank=bank, h=h: e.matmul(pb(bank)[:, 0:128], lhsT=keT[b], rhs=qT4[t3][:, h, :], start=True, stop=True), r=[f"keT{b}", f"qT4{t3}"], w=[bk])
                A("dve", lambda e, b=b, bank=bank: e.tensor_tensor(out=PTm[b], in0=pb(bank)[:, 0:128], in1=tri, op=ALU.mult), r=[bk, "cf"], w=[f"PTm{b}"])
                A("dve", lambda e, h=h: e.tensor_scalar(out=Cst[:, h, :], in0=Cst[:, h, :], scalar1=ex[:, 4 + h:5 + h], scalar2=None, op0=ALU.mult), r=[f"C{h}"], sr=[ek], w=[f"C{h}"])
                A("pool", lambda e, h=h, b=b: e.tensor_copy(out=Cb[b], in_=Cst[:, h, :]), r=[f"C{h}"], w=[f"Cb{b}"])
                A("pe", lambda e, b=b, bank=bank, h=h: e.matmul(pb(bank)[:, 128:257], lhsT=PTm[b], rhs=vb[:, h, :], start=True, stop=False), r=[f"PTm{b}", vk], w=[bk])
                A("pe", lambda e, b=b, bank=bank, h=h: e.matmul(pb(bank)[:, 128:257], lhsT=qT4[t3][:, h, :], rhs=Cb[b], start=False, stop=True), r=[f"qT4{t3}", f"Cb{b}"], w=[bk])
                A("pe", lambda e, b=b, bank=bank, h=h: e.matmul(pb(bank)[:, 258:387], lhsT=ke[b], rhs=vb[:, h, :], start=True, stop=True), r=[f"ke{b}", vk], w=[bk])
                nd = pb(bank)[:, 128:257]
                smb = sm[b]
                smk = f"sm{b}"
                A("act", lambda e, nd=nd, smb=smb: e.activation(out=smb[:, 0:1], in_=nd[:, 128:129], func=AF.Abs), r=[bk], w=[smk])
                A("act", lambda e, nd=nd, smb=smb, b=b: e.activation(out=junk2[:, 0:128], in_=nd[:, 0:128], func=AF.Square, scale=float(128.0 ** -0.5), accum_out=smb[:, 2:3]),
                  r=[bk], w=["junk2", smk])
                A("dve", lambda e, bank=bank, h=h: e.tensor_tensor(out=Cst[:, h, :], in0=pb(bank)[:, 258:387], in1=Cst[:, h, :], op=ALU.add), r=[bk, f"C{h}"], w=[f"C{h}"])
                A("dve", lambda e, h=h, smb=smb: e.tensor_tensor(out=smb[:, 0:1], in0=smb[:, 0:1], in1=ex[:, 8 + h:9 + h], op=ALU.max), r=[smk, ek], w=[smk])
                A("dve", lambda e, smb=smb: e.tensor_tensor(out=smb[:, 1:2], in0=smb[:, 0:1], in1=smb[:, 0:1], op=ALU.mult), r=[smk], w=[smk])
                A("dve", lambda e, smb=smb: e.scalar_tensor_tensor(out=smb[:, 3:4], in0=smb[:, 1:2], scalar=EPS, in1=smb[:, 2:3], op0=ALU.mult, op1=ALU.add), r=[smk], w=[smk])
                A("act", lambda e, smb=smb: e.activation(out=smb[:, 4:5], in_=smb[:, 3:4], func=AF.Ln), r=[smk], w=[smk])
                A("act", lambda e, smb=smb: e.activation(out=smb[:, 6:7], in_=smb[:, 4:5], func=AF.Exp, scale=-0.5), r=[smk], w=[smk])
                A("dve", lambda e, nd=nd, hs=hs, smb=smb: e.scalar_tensor_tensor(out=mix[:, sub, hs], in0=nd[:, 0:128], scalar=smb[:, 6:7], in1=og[t3][:, hs], op0=ALU.mult, op1=ALU.mult),
                  r=[bk, f"og{t3}"], sr=[smk], w=[f"mix{sp_}_{sub}"])

        def L_(t):
            tp = t % 2
            sp_ = (t // QT) % 2
            qTall = qTall2[sp_]
            sub = t % QT
            latt = lat[tp]
            lk = f"lat{tp}"
            A("act", lambda e: e.activation(out=junk3[:, 0:384], in_=latt[:, 8:392], func=AF.Square, scale=float(384.0 ** -0.5), accum_out=ssq[:, 1:2]), r=[lk], w=["junk3", "ssq1"])
            A("act", lambda e: e.activation(out=junk3[:, 0:256], in_=latt[:, 392:648], func=AF.Square, scale=float(256.0 ** -0.5), accum_out=ssq[:, 2:3]), r=[lk], w=["junk3", "ssq1"])
            A("act", lambda e: e.activation(out=rstd2[:, 0:2], in_=ssq[:, 1:3], func=AF.Ln, bias=epst), r=["ssq1", "epst"], w=["rstd1l"])
            A("act", lambda e: e.activation(out=rstd[:, 1:3], in_=rstd2[:, 0:2], func=AF.Exp, scale=-0.5), r=["rstd1l"], w=["rstd1"])
            A("dve", lambda e: e.tensor_scalar(out=lan[:, 0:384], in0=latt[:, 8:392], scalar1=rstd[:, 1:2], scalar2=None, op0=ALU.mult), r=[lk], sr=["rstd1"], w=["lan"])
            A("dve", lambda e: e.tensor_scalar(out=lan[:, 384:640], in0=latt[:, 392:648], scalar1=rstd[:, 2:3], scalar2=None, op0=ALU.mult), r=[lk], sr=["rstd1"], w=["lan"])
            with atomic():
                for j in range(5):
                    A("pe", lambda e, j=j: e.transpose(out=pbb(0)[:, j * 128:(j + 1) * 128], in_=lan[:, j * 128:(j + 1) * 128], identity=identb), r=["lan", "identb"], w=["B0"])
                A("act", lambda e: e.activation(out=lanT.rearrange("p a b -> p (a b)"), in_=pbb(0)[:, 0:640], func=AF.Copy), r=["B0"], w=["lanT"])
            atom[0] += 1
            for (bank, c0, cn) in ((2, 0, 480), (3, 480, 288)):
                for kc in range(3):
                    A("pe", lambda e, bank=bank, c0=c0, cn=cn, kc=kc: e.matmul(pb(bank)[:, 0:cn], lhsT=lanT[:, kc, :], rhs=wqb[:, kc, c0:c0 + cn], start=(kc == 0), stop=(kc == 2)),
                      r=["lanT", f"wqb{kc}"], w=[f"B{bank}"])
            cs = cosT[:, t, :]
            sn = sinT[:, t, :]
            for (bank, h0, nh) in ((2, 0, 5), (3, 5, 3)):
                v = pb(bank)[:, 0:nh * 96].rearrange("p (h c) -> p h c", h=nh)
                bk = f"B{bank}"
                csb = cs.unsqueeze(1).to_broadcast([128, nh, 16])
                snb = sn.unsqueeze(1).to_broadcast([128, nh, 16])
                A("act", lambda e, v=v, h0=h0, nh=nh: e.activation(out=qtm[:, h0:h0 + nh, 0:64], in_=v[:, :, 0:64], func=AF.Copy), r=[bk], w=["qtm"])
                A("dve", lambda e, v=v, nh=nh, csb=csb: e.tensor_tensor(out=rt[0][:, 0:nh, :], in0=v[:, :, 64:80], in1=csb, op=ALU.mult), r=[bk, "cosT"], w=["rt0"])
                A("dve", lambda e, v=v, nh=nh, snb=snb: e.tensor_tensor(out=rt[1][:, 0:nh, :], in0=v[:, :, 80:96], in1=snb, op=ALU.mult), r=[bk, "sinT"], w=["rt1"])
                A("dve", lambda e, v=v, nh=nh, csb=csb: e.tensor_tensor(out=rt[2][:, 0:nh, :], in0=v[:, :, 80:96], in1=csb, op=ALU.mult), r=[bk, "cosT"], w=["rt2"])
                A("dve", lambda e, v=v, nh=nh, snb=snb: e.tensor_tensor(out=rt[3][:, 0:nh, :], in0=v[:, :, 64:80], in1=snb, op=ALU.mult), r=[bk, "sinT"], w=["rt3"])
                A("pool", lambda e, h0=h0, nh=nh: e.tensor_tensor(out=qtm[:, h0:h0 + nh, 64:80], in0=rt[0][:, 0:nh, :], in1=rt[1][:, 0:nh, :], op=ALU.subtract), r=["rt0", "rt1"], w=["qtm"])
                A("pool", lambda e, h0=h0, nh=nh: e.tensor_tensor(out=qtm[:, h0:h0 + nh, 80:96], in0=rt[2][:, 0:nh, :], in1=rt[3][:, 0:nh, :], op=ALU.add), r=["rt2", "rt3"], w=["qtm"])
            atom[0] -= 1
            if cur[0] and cur[0][-1][0] == "open":
                cur[0][-1][0] = "seg"
            kr = latt[:, 648:680]
            A("pool", lambda e: e.tensor_tensor(out=rk[0], in0=kr[:, 0:16], in1=cs, op=ALU.mult), r=[lk, "cosT"], w=["rk0"])
            A("pool", lambda e: e.tensor_tensor(out=rk[1], in0=kr[:, 16:32], in1=sn, op=ALU.mult), r=[lk, "sinT"], w=["rk1"])
            A("pool", lambda e: e.tensor_tensor(out=rk[2], in0=kr[:, 16:32], in1=cs, op=ALU.mult), r=[lk, "cosT"], w=["rk2"])
            A("pool", lambda e: e.tensor_tensor(out=rk[3], in0=kr[:, 0:16], in1=sn, op=ALU.mult), r=[lk, "sinT"], w=["rk3"])
            A("pool", lambda e: e.tensor_tensor(out=kro[:, 0:16], in0=rk[0], in1=rk[1], op=ALU.subtract), r=["rk0", "rk1"], w=["kro"])
            A("pool", lambda e: e.tensor_tensor(out=kro[:, 16:32], in0=rk[2], in1=rk[3], op=ALU.add), r=["rk2", "rk3"], w=["kro"])
            A("pool", lambda e: e.tensor_copy(out=ktm[:, :, 64:96], in_=kro.unsqueeze(1).to_broadcast([128, 8, 32])), r=["kro"], w=["ktm"])
            for (bank, h0) in ((2, 0), (3, 4)):
              with atomic():
                for kc in range(2):
                    A("pe", lambda e, bank=bank, h0=h0, kc=kc: e.matmul(pb(bank), lhsT=lanT[:, 3 + kc, :], rhs=wkvb[:, kc, h0 * 128:h0 * 128 + 512], start=(kc == 0), stop=(kc == 1)),
                      r=["lanT", f"wkvb{kc}"], w=[f"B{bank}"])
                v = pb(bank).rearrange("p (h c) -> p h c", h=4)
                A("act", lambda e, v=v, h0=h0: e.activation(out=ktm[:, h0:h0 + 4, 0:64], in_=v[:, :, 0:64], func=AF.Copy), r=[f"B{bank}"], w=["ktm"])
                A("dve", lambda e, v=v, h0=h0: e.tensor_copy(out=vc[:, t, h0:h0 + 4, 0:64], in_=v[:, :, 64:128]), r=[f"B{bank}"], w=[f"vc{t}"])
            for g in range(2):
              with atomic():
                for j in range(4):
                    h = g * 4 + j
                    A("pe", lambda e, h=h, j=j: e.transpose(out=pbb(0)[0:96, j * 128:(j + 1) * 128], in_=qtm[:, h, :], identity=identb), r=["qtm", "identb"], w=["B0"])
                    A("pe", lambda e, h=h, j=j: e.transpose(out=pbb(0)[0:96, 512 + j * 128:512 + (j + 1) * 128], in_=ktm[:, h, :], identity=identb), r=["ktm", "identb"], w=["B0"])
                A("act", lambda e, g=g: e.activation(out=qTall[0:96, g * 4:g * 4 + 4, sub * 128:(sub + 1) * 128], in_=pbb(0)[0:96, 0:512].rearrange("p (a b) -> p a b", a=4), func=AF.Copy),
                  r=["B0"], w=[f"qTall{sp_}"])
                A("dve", lambda e, g=g: e.tensor_copy(out=kT[0:96, g * 4:g * 4 + 4, t * 128:(t + 1) * 128], in_=pbb(0)[0:96, 512:1024].rearrange("p (a b) -> p a b", a=4)),
                  r=["B0"], w=[f"kT{t}"])

        def attention(stile):
            sp_ = stile % 2
            mix = mix2[sp_]
            qTall = qTall2[sp_]
            npair = stile + 1
            for h in range(8):
                pend = []
                for p in range(npair):
                    sb_ = 4 + (p % 2)
                    bk = f"B{sb_}"
                    pk = p % 3
                    diag = (p == npair - 1)
                    for i in range(2):
                        kt = 2 * p + i
                        q0 = 128 if (diag and i == 1) else 0
                        A("pe", lambda e, sb_=sb_, kt=kt, q0=q0, h=h, i=i: e.matmul(pb(sb_)[:, i * SQ + q0:(i + 1) * SQ], lhsT=kT[0:96, h, kt * 128:(kt + 1) * 128], rhs=qTall[0:96, h, q0:SQ], start=True, stop=True),
                          r=[f"kT{kt}", f"qTall{sp_}"], w=[bk])
                    if not diag:
                        A("act", lambda e, sb_=sb_, pk=pk: e.activation(out=PT[pk], in_=pb(sb_), func=AF.Exp, scale=SC_ATT), r=[bk], w=[f"PT{pk}"])
                    else:
                        A("act", lambda e, sb_=sb_, pk=pk: e.activation(out=PT[pk][:, 0:SQ], in_=pb(sb_)[:, 0:SQ], func=AF.Exp, scale=SC_ATT), r=[bk], w=[f"PT{pk}"])
                        A("act", lambda e, sb_=sb_, pk=pk: e.activation(out=PT[pk][:, SQ + 128:2 * SQ], in_=pb(sb_)[:, SQ + 128:2 * SQ], func=AF.Exp, scale=SC_ATT), r=[bk], w=[f"PT{pk}"])
                        A("pool", lambda e, pk=pk: e.tensor_tensor(out=PT[pk][:, 0:128], in0=PT[pk][:, 0:128], in1=tri_b, op=ALU.mult), r=[f"PT{pk}", "tri_b"], w=[f"PT{pk}"])
                        A("pool", lambda e, pk=pk: e.tensor_tensor(out=PT[pk][:, SQ + 128:2 * SQ], in0=PT[pk][:, SQ + 128:2 * SQ], in1=tri_b, op=ALU.mult), r=[f"PT{pk}", "tri_b"], w=[f"PT{pk}"])
                    while len(pend) > 2:
                        pend.pop(0)()
                    for i in range(2):
                        kt = 2 * p + i
                        q0 = 128 if (diag and i == 1) else 0
                        pend.append(lambda kt=kt, pk=pk, q0=q0, h=h, i=i: A("pe", lambda e: e.matmul(pb(6)[0:65, q0:SQ], lhsT=vc[:, kt, h, :], rhs=PT[pk][:, i * SQ + q0:(i + 1) * SQ], start=(kt == 0), stop=(kt == 2 * npair - 1)),
                                                                        r=[f"vc{kt}", f"PT{pk}"], w=["B6"]))
                for f_ in pend:
                    f_()
                A("dve", lambda e: e.tensor_copy(out=oT[0:65, :], in_=pb(6)[0:65, 0:SQ]), r=["B6"], w=["oT"])
                for sub in range(QT):
                    A("pe", lambda e, sub=sub: e.transpose(out=pb(6)[:, 256 + sub * 65:256 + sub * 65 + 65], in_=oT[0:65, sub * 128:(sub + 1) * 128], identity=identf[0:65, 0:65]),
                      r=["oT", "cf"], w=["B6"])
                for sub in range(QT):
                    o = pb(6)[:, 256 + sub * 65:256 + sub * 65 + 65]
                    A("dve", lambda e, o=o: e.reciprocal(out=rd, in_=o[:, 64:65]), r=["B6"], w=["rd"])
                    A("dve", lambda e, o=o, sub=sub, h=h: e.tensor_scalar(out=mix[:, sub, 512 + h * 64:512 + (h + 1) * 64], in0=o[:, 0:64], scalar1=rd, scalar2=None, op0=ALU.mult),
                      r=["B6"], sr=["rd"], w=[f"mix{sp_}_{sub}"])

        def outproj(stile):
            sp_ = stile % 2
            mix = mix2[sp_]
            for sub in range(QT):
                t = stile * QT + sub
                dma(mix_d[t * 128:(t + 1) * 128, :], mix[:, sub, :], r=[f"mix{sp_}_{sub}"], w=[f"mixd{t}"])
                if dbg:
                    dma(dbg_d[t * 128:(t + 1) * 128, :], mix[:, sub, :], r=[f"mix{sp_}_{sub}"], w=[f"dbgd{t}"])

        def merge_n(lists):
            out = []
            for l in lists:
                out = merge(out, l) if out else list(l)
            return out

        def attn_store(stile):
            attention(stile)
            outproj(stile)

        ntile = nst * QT
        att = {}
        for j in range(ntile + 5):
            streams = []
            if j + 1 < ntile:
                feed(record(Xload, j + 1))
            if j < ntile:
                streams.append(record(P_, j))
            if 0 <= j - 1 < ntile:
                streams.append(record(G_, j - 1))
                streams.append(record(L_, j - 1))
            if 0 <= j - 2 < ntile:
                streams.append(record(H_, j - 2, 0))
                streams.append(record(H_, j - 2, 1))
            if j >= 3 and (j - 3) % 2 == 0 and (j - 3) // 2 < nst:
                st_ = (j - 3) // 2
                full = record(attn_store, st_)
                half = len(full) // 2
                att[j] = full[:half]
                att[j + 1] = full[half:]
            if j in att:
                streams.append(att.pop(j))
            streams = [x for x in streams if x]
            if streams:
                feed(merge_n(streams))
        P.barrier()

        lw_rr = [0]

        def LW(dst, src_d, nk, ncols, gcol0, key):
            npiece = (ncols + 1407) // 1408
            pc = ncols // npiece
            for kc in range(nk):
                for hf in range(npiece):
                    i = lw_rr[0] % 2
                    lw_rr[0] += 1
                    sbuf = stg2[i][:, 0:pc]
                    sk = f"stgb{i}"
                    dma(sbuf, src_d[kc * 128:(kc + 1) * 128, hf * pc:(hf + 1) * pc], w=[sk])
                    sc = None if gcol0 is None or (gcol0 == 13 and kc >= 4) else gains[:, gcol0 + kc:gcol0 + kc + 1]
                    cast(dst[:, kc, hf * pc:(hf + 1) * pc], sbuf, sc, r=[sk], w=[f"{key}{kc}"])

        def LW_out():
            LW(wout, wout_d, 8, D, 13, "wout")
            dma(fnw, fnw_d.partition_broadcast(128), w=["fnw"])

        def LW_gu():
            LW(wg, wg_d, 8, DFF, 17, "wg")
            LW(wu, wu_d, 8, DFF, 17, "wu")

        def LW_d():
            LW(wd, wd_d, 22, D, None, "wd")

        def S1_loads(g, sub):
            t = g * 4 + sub
            par = sub % 2
            dma(h1t[:, sub, :], x_d[t * 128:(t + 1) * 128, :], w=[f"h1t{sub}"])
            dma(hn2[par], mix_d[t * 128:(t + 1) * 128, :], r=[f"mixd{t}"], w=[f"hn{par}"])

        def S1p(g, par):
            tb, hb, ob0, hk_ = (1, 0, 4, "hn0") if par == 0 else (2, 3, 6, "hn1")
            hnb = hn2[par]
            mTb = mixT2[par]
            for sub in (par, par + 2):
                t = g * 4 + sub
                hk = f"h1t{sub}"
                if not (g > 0 and sub < 2):
                    S1_loads(g, sub)
                for kc in range(8):
                    A("pe", lambda e, kc=kc: e.transpose(out=pbb(tb)[:, kc * 128:(kc + 1) * 128], in_=hnb[:, kc * 128:(kc + 1) * 128], identity=identb), r=[hk_, "identb"], w=[f"B{tb}"])
                A("act", lambda e: e.activation(out=mTb.rearrange("p a b -> p (a b)"), in_=pbb(tb), func=AF.Copy), r=[f"B{tb}"], w=[f"mixT{par}"])
                for half in range(2):
                    bank = ob0 + half
                    for kc in range(8):
                        A("pe", lambda e, kc=kc, half=half, bank=bank: e.matmul(pb(bank), lhsT=mTb[:, kc, :], rhs=wout[:, kc, half * 512:(half + 1) * 512], start=(kc == 0), stop=(kc == 7)),
                          r=[f"mixT{par}", f"wout{kc}"], w=[f"B{bank}"])
                    A("dve", lambda e, half=half, bank=bank, sub=sub: e.tensor_tensor(out=h1t[:, sub, half * 512:(half + 1) * 512], in0=pb(bank), in1=h1t[:, sub, half * 512:(half + 1) * 512], op=ALU.add),
                      r=[f"B{bank}", hk], w=[hk])
                A("act", lambda e, sub=sub: e.activation(out=hnb, in_=h1t[:, sub, :], func=AF.Square, scale=float(D ** -0.5), accum_out=ssq[:, par:par + 1]), r=[hk], w=[hk_, f"ssq_{par}"])
                A("act", lambda e: e.activation(out=rstd2[:, par:par + 1], in_=ssq[:, par:par + 1], func=AF.Ln, bias=epst), r=[f"ssq_{par}"], sr=["epst"], w=[f"rstdl_{par}"])
                A("act", lambda e: e.activation(out=rstd[:, par:par + 1], in_=rstd2[:, par:par + 1], func=AF.Exp, scale=-0.5), r=[f"rstdl_{par}"], w=[f"rstd_{par}"])
                A("dve", lambda e, sub=sub: e.tensor_scalar(out=hnb, in0=h1t[:, sub, :], scalar1=rstd[:, par:par + 1], scalar2=None, op0=ALU.mult), r=[hk], sr=[f"rstd_{par}"], w=[hk_])
                for kc in range(8):
                    A("pe", lambda e, kc=kc: e.transpose(out=pbb(hb)[:, kc * 128:(kc + 1) * 128], in_=hnb[:, kc * 128:(kc + 1) * 128], identity=identb), r=[hk_, "identb"], w=[f"B{hb}"])
                A("act", lambda e, sub=sub: e.activation(out=hnT[:, :, sub * 128:(sub + 1) * 128], in_=pbb(hb).rearrange("p (a b) -> p a b", a=8), func=AF.Copy), r=[f"B{hb}"], w=[f"hnT{sub}"])

        def S2_(g):
            for fc in range(22):
                gb = 2 + 2 * (fc % 2)
                ub = gb + 1
                fs = slice(fc * 128, (fc + 1) * 128)
                for kc in range(8):
                    A("pe", lambda e, kc=kc, fs=fs, gb=gb: e.matmul(pb(gb), lhsT=wg[:, kc, fs], rhs=hnT[:, kc, :], start=(kc == 0), stop=(kc == 7)), r=[f"wg{kc}", "hnT0", "hnT1", "hnT2", "hnT3"], w=[f"B{gb}"])
                for kc in range(8):
                    A("pe", lambda e, kc=kc, fs=fs, ub=ub: e.matmul(pb(ub), lhsT=wu[:, kc, fs], rhs=hnT[:, kc, :], start=(kc == 0), stop=(kc == 7)), r=[f"wu{kc}", "hnT0", "hnT1", "hnT2", "hnT3"], w=[f"B{ub}"])
                sgb = sg[fc % 2]
                A("act", lambda e, gb=gb, sgb=sgb: e.activation(out=sgb, in_=pb(gb), func=AF.Silu), r=[f"B{gb}"], w=["sg"])
                A("dve", lambda e, ub=ub, sgb=sgb, fc=fc: e.tensor_tensor(out=actT[:, fc, :], in0=pb(ub), in1=sgb, op=ALU.mult), r=[f"B{ub}", "sg"], w=[f"actT{fc}"])

        def S3_(g):
            for sub in range(4):
                t = g * 4 + sub
                for half in range(2):
                    bank = 6 + half
                    for fc in range(22):
                        A("pe", lambda e, fc=fc, sub=sub, half=half, bank=bank: e.matmul(pb(bank), lhsT=actT[:, fc, sub * 128:(sub + 1) * 128], rhs=wd[:, fc, half * 512:(half + 1) * 512], start=(fc == 0), stop=(fc == 21)),
                          r=[f"actT{fc}", f"wd{fc}"], w=[f"B{bank}"])
                    A("dve", lambda e, sub=sub, half=half, bank=bank: e.tensor_tensor(out=h1t[:, sub, half * 512:(half + 1) * 512], in0=pb(bank), in1=h1t[:, sub, half * 512:(half + 1) * 512], op=ALU.add),
                      r=[f"B{bank}", f"h1t{sub}"], w=[f"h1t{sub}"])
                A("act", lambda e, sub=sub: e.activation(out=mixT2[1].rearrange("p a b -> p (a b)"), in_=h1t[:, sub, :], func=AF.Square, scale=float(D ** -0.5), accum_out=ssq[:, 3:4]), r=[f"h1t{sub}"], w=["mixT1", "ssq3"])
                A("act", lambda e: e.activation(out=rstd2[:, 3:4], in_=ssq[:, 3:4], func=AF.Ln, bias=epst), r=["ssq3", "epst"], w=["rstd3l"])
                A("act", lambda e: e.activation(out=rstd[:, 3:4], in_=rstd2[:, 3:4], func=AF.Exp, scale=-0.5), r=["rstd3l"], w=["rstd3"])
                A("dve", lambda e, sub=sub: e.scalar_tensor_tensor(out=h1t[:, sub, :], in0=h1t[:, sub, :], scalar=rstd[:, 3:4], in1=fnw, op0=ALU.mult, op1=ALU.mult),
                  r=[f"h1t{sub}", "fnw"], sr=["rstd3"], w=[f"h1t{sub}"])
                dma(out_d[t * 128:(t + 1) * 128, :], h1t[:, sub, :], r=[f"h1t{sub}"], w=[f"outd{t}"])
                if sub < 2 and g + 1 < S // 512:
                    S1_loads(g + 1, sub)


        if do_p2:
            feed(record(LW_out))
            for g in range(S // 512):
                s1 = merge(record(S1p, g, 0), record(S1p, g, 1))
                if g == 0:
                    feed(merge(s1, record(LW_gu)))
                    feed(merge(record(S2_, g), record(LW_d)))
                else:
                    feed(s1)
                    feed(record(S2_, g))
                feed(record(S3_, g))

        P.emit(nc, st)
    return nc


_CONSTS = None


def _consts():
    global _CONSTS
    if _CONSTS is None:
        identb = np.eye(128, dtype=np.float32).astype(ml_dtypes.bfloat16)
        identf = np.eye(128, dtype=np.float32)
        tri = np.triu(np.ones((128, 128), dtype=np.float32))
        ones = np.ones((128, 128), dtype=np.float32)
        _CONSTS = (identb, np.ascontiguousarray(np.concatenate([identf, tri, ones], axis=1)))
    return _CONSTS


def kernel(x, positions, attn_norm_w, w_in, b_gates, mlstm_norm_w, q_a_norm_w, w_q_b,
           kv_a_norm_w, w_kv_b, w_out, ffn_norm_w, w_gate, w_up, w_down, final_norm_w):
    f = lambda a: np.ascontiguousarray(np.asarray(a, dtype=np.float32))
    x = f(x)
    positions = np.asarray(positions, dtype=np.int32)
    identb, cf = _consts()
    col = lambda v, n: f(v).reshape(n, 128).T
    gains = np.ascontiguousarray(np.concatenate([
        col(attn_norm_w[0], 8), col(q_a_norm_w[0], 3), col(kv_a_norm_w[0], 2),
        col(mlstm_norm_w[0].reshape(-1), 4), col(ffn_norm_w[0], 8)], axis=1))
    shared = {
        "w_in": f(w_in[0]), "w_q_b": f(w_q_b[0]), "w_kv_b": f(w_kv_b[0]), "w_out": f(w_out[0]),
        "w_gate": f(w_gate[0]), "w_up": f(w_up[0]), "w_down": f(w_down[0]),
        "gains": gains, "final_nw": f(final_norm_w).reshape(1, D), "b_gates": f(b_gates).reshape(1, 8),
        "identb": identb, "cf": cf,
    }
    nc = build_program()
    in_maps = []
    for b in range(8):
        m = dict(shared)
        m["x"] = x[b]
        m["pos"] = np.ascontiguousarray(positions[b].reshape(NT, 128).T)
        in_maps.append(m)
    res = run_bass_kernel_spmd(nc, in_maps, core_ids=list(range(8)))
    return np.stack([np.asarray(r["out"], dtype=np.float32) for r in res.results], axis=0)
```

```python
import math
import numpy as np
import ml_dtypes
from contextlib import ExitStack
import concourse.bass as bass
import concourse.mybir as mybir
from concourse.bass_utils import run_bass_kernel_spmd

F32 = mybir.dt.float32
BF16 = mybir.dt.bfloat16
I32 = mybir.dt.int32
U8 = mybir.dt.uint8
AF = mybir.ActivationFunctionType
ALU = mybir.AluOpType
AX = mybir.AxisListType

S = 4096
D = 1024
DIN = 2728
DFF = 2816
NT = S // 128
QT = 2
NST = NT // QT
SQ = QT * 128
EPS = 1e-6
GEN = 20000


class Op:
    __slots__ = ("eng", "fn", "deps", "is_dma", "sig", "sem", "val", "prev", "eidx")

    def __init__(self, eng, fn, is_dma):
        self.eng = eng
        self.fn = fn
        self.is_dma = is_dma
        self.deps = []
        self.sig = False
        self.sem = None
        self.val = 0
        self.prev = None


class Prog:
    NDMA = 24
    WINDOW = 16
    STRICT = True

    def __init__(self):
        self.ops = []
        self.last_w = {}
        self.readers = {}
        self.last_eng = {}
        self.dma_recent = {}
        self.pending_barrier = {}
        self.ecount = {}
        self.t_last_w = {}
        self.t_readers = {}

    def add(self, eng, fn, r=(), w=(), dma=False, sr=()):
        op = Op(eng, fn, dma)
        deps = []
        seen = set()
        isb = lambda k: len(k) >= 2 and k[0] == "B" and k[1].isdigit()
        nb = lambda k: k[:2] if isb(k) else k
        tr = list(dict.fromkeys([nb(k) for k in list(r) + list(sr)]))
        tw = list(dict.fromkeys([nb(k) for k in w]))
        w = list(dict.fromkeys([nb(k) for k in list(w) + [k for k in r if isb(k)]]))
        r = [k for k in r if not isb(k)] + list(sr)
        eidx = self.ecount.get(eng, 0)
        self.ecount[eng] = eidx + 1
        op.eidx = eidx

        def push(d, raw):
            if id(d) in seen:
                return
            if d.is_dma or d.eng != eng or dma:
                seen.add(id(d))
                deps.append(d)
            elif raw and eng != "pe" and eidx - d.eidx <= self.WINDOW:
                seen.add(id(d))
                deps.append(d)

        for k in r:
            d = self.last_w.get(k)
            if d is not None:
                push(d, True)
        for k in w:
            d = self.last_w.get(k)
            if d is not None:
                push(d, isb(k) and k in tr)
            for d in self.readers.get(k, ()):
                push(d, False)
        if self.STRICT:
            def spush(d):
                if id(d) not in seen and d.eng == eng and not d.is_dma and not (eng == "pe"):
                    seen.add(id(d))
                    deps.append(d)
            for k in tr:
                d = self.t_last_w.get(k)
                if d is not None:
                    spush(d)
            for k in tw:
                d = self.t_last_w.get(k)
                if d is not None:
                    spush(d)
                for d in self.t_readers.get(k, ()):
                    spush(d)
        for k in tw:
            self.t_last_w[k] = op
            self.t_readers[k] = []
        for k in tr:
            self.t_readers.setdefault(k, []).append(op)
        if eng in self.pending_barrier:
            for d in self.pending_barrier.pop(eng):
                push(d, False)
        for k in w:
            self.last_w[k] = op
            self.readers[k] = []
        for k in r:
            self.readers.setdefault(k, []).append(op)
        op.deps = deps
        for d in deps:
            d.sig = True
        self.ops.append(op)
        self.last_eng[eng] = op
        if dma:
            lst = self.dma_recent.setdefault(eng, [])
            lst.append(op)
            if len(lst) > self.NDMA:
                lst.pop(0)
        return op

    def barrier(self):
        deps = [o for o in self.last_eng.values() if not o.is_dma]
        for lst in self.dma_recent.values():
            deps.extend(lst)
        for e in ("pe", "act", "dve", "pool", "sp"):
            self.pending_barrier[e] = list(deps)

    def emit(self, nc, stack):
        names = {"pe": "tensor", "act": "scalar", "dve": "vector", "pool": "gpsimd", "sp": "sync"}
        per = {e: [o for o in self.ops if o.eng == e] for e in names}
        sems = {}

        def getsem(name):
            if name not in sems:
                sems[name] = stack.enter_context(nc.semaphore(name))
            return sems[name]

        for e, lst in per.items():
            cnt = 0
            dcnt = 0
            slot_last = {}
            for o in lst:
                if o.is_dma:
                    slot = dcnt % self.NDMA
                    o.sem = getsem(f"d_{e}_{slot}")
                    o.prev = slot_last.get(slot)
                    o.val = (o.prev.val if o.prev is not None else 0) + 16
                    slot_last[slot] = o
                    dcnt += 1
                elif o.sig:
                    o.sem = getsem(f"c_{e}_{cnt // GEN}")
                    o.val = cnt % GEN + 1
                    cnt += 1
        block = stack.enter_context(nc.Block())

        def run(e, lst):
            def body(eng):
                waited = {}
                for o in lst:
                    ws = [(d.sem, d.val) for d in o.deps]
                    if o.is_dma and o.prev is not None:
                        ws.append((o.prev.sem, o.prev.val))
                    for s, v in ws:
                        if waited.get(s.name, 0) < v:
                            eng.wait_ge(s, v)
                            waited[s.name] = v
                    ins = o.fn(eng)
                    if o.is_dma:
                        ins.then_inc(o.sem, 16)
                    elif o.sig:
                        ins.then_inc(o.sem, 1)
                last = {}
                for o in lst:
                    if o.is_dma:
                        last[o.sem.name] = (o.sem, o.val)
                for s, v in last.values():
                    if waited.get(s.name, 0) < v:
                        eng.wait_ge(s, v)
            return body

        for e, lst in per.items():
            if lst:
                getattr(block, names[e])(run(e, lst))


def build_program(nst=NST, do_p2=True, dbg=False, lvl=99):
    nc = bass.Bass("TRN2", target_bir_lowering=False, dynamic_dma_scratch_size=256)
    dt = lambda n, s, d, k="ExternalInput": nc.dram_tensor(n, s, d, kind=k).ap()
    x_d = dt("x", [S, D], F32)
    pos_d = dt("pos", [128, NT], I32)
    w_in_d = dt("w_in", [D, DIN], F32)
    wqb_d = dt("w_q_b", [384, 768], F32)
    wkvb_d = dt("w_kv_b", [256, 1024], F32)
    wout_d = dt("w_out", [D, D], F32)
    wg_d = dt("w_gate", [D, DFF], F32)
    wu_d = dt("w_up", [D, DFF], F32)
    wd_d = dt("w_down", [DFF, D], F32)
    gains_d = dt("gains", [128, 25], F32)
    fnw_d = dt("final_nw", [1, D], F32)
    bg_d = dt("b_gates", [1, 8], F32)
    identb_d = dt("identb", [128, 128], BF16)
    cf_d = dt("cf", [128, 384], F32)
    out_d = dt("out", [S, D], F32, "ExternalOutput")
    mix_d = dt("mixs", [S, D], BF16, "Internal")
    dbg_d = dt("dbg", [S, D], BF16, "ExternalOutput") if dbg else None
    dbgf_d = dt("dbgf", [128, 1024], F32, "ExternalOutput") if dbg else None

    P = Prog()
    inv_freq = (np.float32(10000.0) ** (-np.arange(16, dtype=np.float32) / np.float32(16))).astype(np.float32)

    with ExitStack() as st:
        ARENA = 222 * 1024
        arena = st.enter_context(nc.sbuf_tensor("arena", [128, ARENA], U8))
        banks = [st.enter_context(nc.psum_tensor(f"bank{i}", [128, 512], F32)) for i in range(8)]
        off = [0]

        def alloc(shape, dtype, at=None):
            nb = int(np.prod(shape[1:])) * mybir.dt.size(dtype)
            nb = (nb + 63) // 64 * 64
            o = off[0] if at is None else at
            if at is None:
                off[0] += nb
            assert o + nb <= ARENA, (o, nb)
            ap = arena[:, o:o + nb].bitcast(dtype)
            n = int(np.prod(shape[1:]))
            ap = ap[:, 0:n]
            if len(shape) == 3:
                ap = ap.rearrange("p (a b) -> p a b", a=shape[1])
            elif len(shape) == 4:
                ap = ap.rearrange("p (a b c) -> p a b c", a=shape[1], b=shape[2])
            return ap

        def pb(i):
            return banks[i][:]

        def pbb(i):
            return banks[i][:].bitcast(BF16)

        identb = alloc([128, 128], BF16)
        cf = alloc([128, 384], F32)
        identf = cf[:, 0:128]
        tri = cf[:, 128:256]
        ones = cf[:, 256:384]
        gains = alloc([128, 25], F32)
        epst = alloc([128, 1], F32)
        onec = alloc([128, 1], F32)
        ssq = alloc([128, 4], F32)
        rstd = alloc([128, 4], F32)
        rstd2 = alloc([128, 4], F32)
        common_end = off[0]
        junk_at = off[0]
        junk = alloc([128, 1024], BF16)

        kT_at = off[0]
        kT = alloc([128, 8, S], BF16)
        vc_at = off[0]
        vc = alloc([128, NT, 8, 65], BF16)
        w_in = alloc([128, 8, DIN], BF16)
        wqb = alloc([128, 3, 768], BF16)
        wkvb = alloc([128, 2, 1024], BF16)
        cosT = alloc([128, NT, 16], F32)
        sinT = alloc([128, NT, 16], F32)
        bgate = alloc([128, 8], F32)
        tri_b = alloc([128, 128], BF16)
        Cst = alloc([128, 4, 129], F32)
        mprev = [alloc([128, 4], F32) for _ in range(2)]
        vext = [alloc([128, 4, 129], BF16) for _ in range(3)]
        p1_misc = off[0]
        stage_at = off[0]
        xring = [alloc([128, D], F32) for _ in range(2)]
        u_t = alloc([128, D], BF16)
        uT = alloc([128, 8, 128], BF16)
        q_tm = [alloc([128, 512], BF16) for _ in range(2)]
        qT4 = [alloc([128, 4, 128], BF16) for _ in range(3)]
        k_tm = [alloc([128, 512], BF16) for _ in range(3)]
        og = [alloc([128, 512], BF16) for _ in range(3)]
        ogt = alloc([128, 512], F32, at=junk_at)
        lat = [alloc([128, 680], F32) for _ in range(2)]
        gts = [alloc([128, 8], F32) for _ in range(2)]
        ef = [alloc([128, 4], F32) for _ in range(2)]
        spl = [alloc([128, 12], F32) for _ in range(2)]
        apr = [alloc([128, 4], F32) for _ in range(2)]
        cst_ = [alloc([128, 4], F32) for _ in range(2)]
        Mx = [alloc([128, 4], F32) for _ in range(2)]
        args = [alloc([128, 12], F32) for _ in range(2)]
        exs = [alloc([128, 12], F32) for _ in range(2)]
        ke = [alloc([128, 128], BF16) for _ in range(2)]
        keT = [alloc([128, 128], BF16) for _ in range(2)]
        PTm = [alloc([128, 128], BF16) for _ in range(2)]
        Cb = [alloc([128, 129], BF16) for _ in range(2)]
        sm = [alloc([128, 8], F32) for _ in range(2)]
        junk2 = alloc([128, 128], BF16)
        junk3 = alloc([128, 384], BF16)
        rk = [alloc([128, 16], F32) for _ in range(4)]
        lan = alloc([128, 640], BF16)
        lanT = alloc([128, 5, 128], BF16)
        qtm = alloc([128, 8, 96], BF16)
        ktm = alloc([128, 8, 96], BF16)
        rt = [alloc([128, 8, 16], F32) for _ in range(4)]
        kro = alloc([128, 32], BF16)
        qTall2 = [alloc([128, 8, SQ], BF16) for _ in range(2)]
        PT = [alloc([128, 2 * SQ], BF16) for _ in range(3)]
        oT = alloc([128, SQ], F32)
        mix2 = [alloc([128, QT, D], BF16) for _ in range(2)]
        rd = alloc([128, 1], F32)
        p1_end = off[0]
        VT0 = 10
        stg = [alloc([128, DFF], F32, at=vc_at + VT0 * 1040 + i * 11264) for i in range(2)]
        assert VT0 * 1040 + 2 * 11264 <= NT * 1040
        posi = alloc([128, NT], I32, at=kT_at)
        posf = alloc([128, NT], F32, at=kT_at + 128)
        ang = alloc([128, NT, 32], F32, at=kT_at + 256)
        ki = alloc([128, NT * 32], I32, at=kT_at + 256 + 4096)
        kf = alloc([128, NT * 32], F32, at=kT_at + 256 + 8192)

        off[0] = common_end
        wg = alloc([128, 8, DFF], BF16)
        wu = alloc([128, 8, DFF], BF16)
        wd = alloc([128, 22, D], BF16)
        wout = alloc([128, 8, D], BF16)
        fnw = alloc([128, D], F32)
        h1t = alloc([128, 4, D], F32)
        hn2 = [alloc([128, D], BF16) for _ in range(2)]
        hn = hn2[0]
        mixT2 = [alloc([128, 8, 128], BF16) for _ in range(2)]
        hnT = alloc([128, 8, 512], BF16)
        sg_ = alloc([128, 512], BF16)
        sg = [sg_, sg_]
        act_at = off[0]
        actT = alloc([128, 22, 512], BF16)
        stg2 = [alloc([128, 1408], F32) for i in range(2)]
        p2_end = off[0]

        cur = [None]
        atom = [0]

        def A(eng, fn, r=(), w=(), dma=False, sr=()):
            if cur[0] is not None:
                op = (eng, fn, tuple(r), tuple(w), dma, tuple(sr))
                if atom[0] and cur[0] and cur[0][-1][0] == "open":
                    cur[0][-1][1].append(op)
                elif atom[0]:
                    cur[0].append(["open", [op]])
                else:
                    cur[0].append(["seg", [op]])
            else:
                P.add(eng, fn, r=r, w=w, dma=dma, sr=sr)

        class atomic:
            def __enter__(self):
                atom[0] += 1

            def __exit__(self, *a_):
                atom[0] -= 1
                if atom[0] == 0 and cur[0] and cur[0][-1][0] == "open":
                    cur[0][-1][0] = "seg"

        def record(f, *args):
            cur[0] = []
            f(*args)
            lst = [seg[1] for seg in cur[0]]
            cur[0] = None
            return lst

        def feed(segs):
            for seg in segs:
                for o in seg:
                    P.add(o[0], o[1], r=o[2], w=o[3], dma=o[4], sr=o[5])

        def merge(l1, l2):
            n1 = sum(len(x) for x in l1)
            n2 = sum(len(x) for x in l2)
            out = []
            i = j = 0
            c1 = c2 = 0
            while i < len(l1) or j < len(l2):
                if j >= len(l2) or (i < len(l1) and (c1 + 0.5 * len(l1[i])) * n2 <= (c2 + 0.5 * len(l2[j])) * n1):
                    out.append(l1[i]); c1 += len(l1[i]); i += 1
                else:
                    out.append(l2[j]); c2 += len(l2[j]); j += 1
            return out

        cast_rr = [0]

        def cast(out, in_, scale, r, w):
            i = cast_rr[0]
            cast_rr[0] += 1
            rr = list(r) + (["gains"] if scale is not None else [])
            if i % 2 == 0:
                if scale is None:
                    A("act", lambda e: e.activation(out=out, in_=in_, func=AF.Copy), r=rr, w=w)
                else:
                    A("act", lambda e: e.activation(out=out, in_=in_, func=AF.Identity, scale=scale), r=rr, w=w)
            else:
                if scale is None:
                    A("dve", lambda e: e.tensor_copy(out=out, in_=in_), r=rr, w=w)
                else:
                    A("dve", lambda e: e.tensor_scalar(out=out, in0=in_, scalar1=scale, scalar2=None, op0=ALU.mult), r=rr, w=w)

        def dma(out, in_, r=(), w=()):
            A("sp", lambda e: e.dma_start(out=out, in_=in_), r=r, w=w, dma=True)

        def dump(ap, c0, n, rk):
            if dbg:
                npart = ap.shape[0]
                A("sp", lambda e: e.dma_start(out=dbgf_d[0:npart, c0:c0 + n], in_=ap, allow_slow_non_contiguous=True), r=rk, w=[f"dbgf{c0}"], dma=True)

        def evac(i, out, in_, r, w):
            if i % 2 == 0:
                A("act", lambda e: e.activation(out=out, in_=in_, func=AF.Copy), r=r, w=w)
            else:
                A("dve", lambda e: e.tensor_copy(out=out, in_=in_), r=r, w=w)

        dma(identb, identb_d[:, :], w=["identb"])
        dma(cf, cf_d[:, :], w=["cf"])
        dma(gains, gains_d[:, :], w=["gains"])
        dma(bgate, bg_d.partition_broadcast(128), w=["bgate"])
        dma(posi, pos_d[:, :], w=["posi"])
        A("dve", lambda e: e.memset(epst, EPS), w=["epst"])
        A("dve", lambda e: e.memset(onec, 1.0), w=["onec"])
        A("dve", lambda e: e.tensor_copy(out=tri_b, in_=tri), r=["cf"], w=["tri_b"])
        A("dve", lambda e: e.memset(Cst.rearrange("p a b -> p (a b)"), 0.0), w=["C0", "C1", "C2", "C3"])
        A("dve", lambda e: e.memset(mprev[0], 0.0), w=["mprev0"])
        A("pool", lambda e: e.memset(vc[:, 0:VT0].rearrange("p a b c -> p (a b c)"), 1.0), w=[f"vc{t}" for t in range(VT0)])
        for b in range(3):
            A("pool", lambda e, b=b: e.memset(vext[b].rearrange("p a b -> p (a b)"), 1.0), w=[f"vext{b}"])

        A("dve", lambda e: e.tensor_copy(out=posf, in_=posi), r=["posi"], w=["posf"])
        for j in range(16):
            A("dve", lambda e, j=j: e.tensor_scalar(out=ang[:, :, j], in0=posf, scalar1=float(inv_freq[j]), scalar2=None, op0=ALU.mult),
              r=["posf"], w=["ang"])
        A("dve", lambda e: e.tensor_scalar(out=ang[:, :, 16:32], in0=ang[:, :, 0:16], scalar1=float(np.pi / 2), scalar2=None, op0=ALU.add),
          r=["ang"], w=["ang"])
        angf = ang.rearrange("p a b -> p (a b)")
        TWO_PI = float(2 * np.pi)
        A("dve", lambda e: e.tensor_scalar(out=ki, in0=angf, scalar1=float(1.0 / TWO_PI), scalar2=None, op0=ALU.mult), r=["ang"], w=["ki"])
        A("dve", lambda e: e.tensor_copy(out=kf, in_=ki), r=["ki"], w=["kf"])
        A("dve", lambda e: e.scalar_tensor_tensor(out=angf, in0=kf, scalar=-TWO_PI, in1=angf, op0=ALU.mult, op1=ALU.add), r=["kf", "ang"], w=["ang"])
        A("dve", lambda e: e.tensor_scalar(out=kf, in0=angf, scalar1=float(np.pi), scalar2=-TWO_PI, op0=ALU.is_gt, op1=ALU.mult), r=["ang"], w=["kf"])
        A("dve", lambda e: e.tensor_tensor(out=angf, in0=angf, in1=kf, op=ALU.add), r=["ang", "kf"], w=["ang"])
        A("dve", lambda e: e.tensor_scalar(out=kf, in0=angf, scalar1=float(-np.pi), scalar2=TWO_PI, op0=ALU.is_lt, op1=ALU.mult), r=["ang"], w=["kf"])
        A("dve", lambda e: e.tensor_tensor(out=angf, in0=angf, in1=kf, op=ALU.add), r=["ang", "kf"], w=["ang"])
        A("dve", lambda e: e.tensor_scalar(out=angf, in0=angf, scalar1=3.14159, scalar2=-3.14159, op0=ALU.min, op1=ALU.max), r=["ang"], w=["ang"])
        A("act", lambda e: e.activation(out=sinT, in_=ang[:, :, 0:16], func=AF.Sin), r=["ang"], w=["sinT"])
        A("act", lambda e: e.activation(out=cosT, in_=ang[:, :, 16:32], func=AF.Sin), r=["ang"], w=["cosT"])

        def load_w(dst, src_d, nk, ncols, gcol0, key, stgs, skey):
            for kc in range(nk):
                s = stgs[kc % 2][:, 0:ncols]
                sk = f"{skey}{kc % 2}"
                dma(s, src_d[kc * 128:(kc + 1) * 128, :], w=[sk])
                sc = None if gcol0 is None or (gcol0 == 13 and kc >= 4) else gains[:, gcol0 + kc:gcol0 + kc + 1]
                cast(dst[:, kc, :], s, sc, r=[sk], w=[f"{key}{kc}"])

        load_w(w_in, w_in_d, 8, DIN, 0, "w_in", stg, "stg")
        load_w(wqb, wqb_d, 3, 768, 8, "wqb", stg, "stg")
        load_w(wkvb, wkvb_d, 2, 1024, 11, "wkvb", stg, "stg")
        A("pool", lambda e: e.memset(vc[:, VT0:NT].rearrange("p a b c -> p (a b c)"), 1.0), w=["stg0", "stg1"] + [f"vc{t}" for t in range(VT0, NT)])

        WIN = [f"w_in{k}" for k in range(8)]
        SC_ATT = float(96.0 ** -0.5)

        def Xload(t):
            dma(xring[t % 2], x_d[t * 128:(t + 1) * 128, :], w=[f"x{t % 2}"])

        def P_(t):
            tp = t % 2
            t3 = t % 3
            xt = xring[tp]
            xk = f"x{tp}"
            if t == 0:
                Xload(0)
            A("act", lambda e: e.activation(out=junk, in_=xt, func=AF.Square, scale=float(D ** -0.5), accum_out=ssq[:, 0:1]),
              r=[xk], w=["junk", "ssq"])
            A("act", lambda e: e.activation(out=rstd[:, 3:4], in_=ssq[:, 0:1], func=AF.Ln, bias=epst), r=["ssq"], sr=["epst"], w=["rstdl"])
            A("act", lambda e: e.activation(out=rstd[:, 0:1], in_=rstd[:, 3:4], func=AF.Exp, scale=-0.5), r=["rstdl"], w=["rstd"])
            A("dve", lambda e: e.tensor_scalar(out=u_t, in0=xt, scalar1=rstd[:, 0:1], scalar2=None, op0=ALU.mult), r=[xk], sr=["rstd"], w=["u"])
            with atomic():
                for kc in range(8):
                    A("pe", lambda e, kc=kc: e.transpose(out=pbb(0)[:, kc * 128:(kc + 1) * 128], in_=u_t[:, kc * 128:(kc + 1) * 128], identity=identb),
                      r=["u", "identb"], w=["B0"])
                A("dve", lambda e: e.tensor_copy(out=uT.rearrange("p a b -> p (a b)"), in_=pbb(0)), r=["B0"], w=["uT"])

            def proj(bank, c0, cn):
                for kc in range(8):
                    A("pe", lambda e, kc=kc: e.matmul(pb(bank)[:, 0:cn], lhsT=uT[:, kc, :], rhs=w_in[:, kc, c0:c0 + cn], start=(kc == 0), stop=(kc == 7)),
                      r=["uT", WIN[kc]], w=[f"B{bank}"])

            vb = vext[t3]
            with atomic():
                proj(2, 0, 512)
                A("act", lambda e: e.activation(out=q_tm[tp], in_=pb(2), func=AF.Copy, scale=float(128.0 ** -0.5)), r=["B2"], w=[f"q_tm{tp}"])
            with atomic():
                proj(3, 512, 512)
                A("dve", lambda e: e.tensor_copy(out=k_tm[t3], in_=pb(3)), r=["B3"], w=[f"k_tm{t3}"])
            with atomic():
                proj(2, 1024, 512)
                A("act", lambda e: e.activation(out=vb[:, :, 0:128], in_=pb(2).rearrange("p (a b) -> p a b", a=4), func=AF.Copy), r=["B2"], w=[f"vext{t3}"])
            with atomic():
                proj(3, 1536, 512)
                A("act", lambda e: e.activation(out=ogt, in_=pb(3), func=AF.Exp, scale=-1.0), r=["B3"], w=["junk"])
            A("dve", lambda e: e.tensor_scalar(out=ogt, in0=ogt, scalar1=1.0, scalar2=None, op0=ALU.add), r=["junk"], w=["junk"])
            def _rcp(e):
                with nc.allow_low_precision("output gate is stored in bf16 (it only scales the bf16 mixer output)"):
                    return e.reciprocal(out=og[t3], in_=ogt)
            A("dve", _rcp, r=["junk"], w=[f"og{t3}"])
            with atomic():
                proj(2, 2048, 512)
                A("dve", lambda e: e.tensor_copy(out=lat[tp][:, 0:512], in_=pb(2)), r=["B2"], w=[f"lat{tp}"])
            with atomic():
                proj(3, 2560, 168)
                A("dve", lambda e: e.tensor_copy(out=lat[tp][:, 512:680], in_=pb(3)[:, 0:168]), r=["B3"], w=[f"lat{tp}"])
            with atomic():
                for h in range(4):
                    A("pe", lambda e, h=h: e.transpose(out=pbb(0)[:, h * 128:(h + 1) * 128], in_=q_tm[tp][:, h * 128:(h + 1) * 128], identity=identb),
                      r=[f"q_tm{tp}", "identb"], w=["B0"])
                A("dve", lambda e: e.tensor_copy(out=qT4[t3].rearrange("p a b -> p (a b)"), in_=pbb(0)[:, 0:512]), r=["B0"], w=[f"qT4{t3}"])

        def G_(t):
            tp = t % 2
            latt = lat[tp]
            g_, sp, ap_, mx, ar, ex, cc = gts[tp], spl[tp], apr[tp], Mx[tp], args[tp], exs[tp], cst_[tp]
            mp = mprev[tp]
            mn = mprev[1 - tp]
            mpk = f"mprev{tp}"
            mnk = f"mprev{1 - tp}"
            gk = f"g{tp}"
            A("dve", lambda e: e.tensor_tensor(out=g_, in0=latt[:, 0:8], in1=bgate, op=ALU.add), r=[f"lat{tp}", "bgate"], w=[gk + "gts"])
            A("act", lambda e: e.activation(out=ef[tp], in_=g_[:, 4:8], func=AF.Exp, scale=-1.0), r=[gk + "gts"], w=[gk + "ef"])
            A("act", lambda e: e.activation(out=g_[:, 4:8], in_=ef[tp], func=AF.Ln, bias=onec), r=[gk + "ef", "onec"], w=[gk + "gts"])
            with atomic():
                A("pe", lambda e: e.matmul(pb(0)[:, 0:4], lhsT=tri, rhs=g_[:, 4:8], start=True, stop=True), r=["cf", gk + "gts"], w=["B0"])
                A("pe", lambda e: e.matmul(pb(0)[:, 4:12], lhsT=ones, rhs=g_, start=True, stop=True), r=["cf", gk + "gts"], w=["B0"])
                A("dve", lambda e: e.tensor_copy(out=sp, in_=pb(0)[:, 0:12]), r=["B0"], w=[gk + "sp"])
            A("dve", lambda e: e.tensor_tensor(out=ap_, in0=sp[:, 0:4], in1=g_[:, 0:4], op=ALU.add), r=[gk + "sp", gk + "gts"], w=[gk + "apr"])
            A("dve", lambda e: e.tensor_scalar(out=cc, in0=sp[:, 8:12], scalar1=0.5, scalar2=None, op0=ALU.mult), r=[gk + "sp"], w=[gk + "cc"])
            A("dve", lambda e: e.scalar_tensor_tensor(out=cc, in0=sp[:, 4:8], scalar=float(1.0 / 128), in1=cc, op0=ALU.mult, op1=ALU.add), r=[gk + "sp", gk + "cc"], w=[gk + "cc"])
            A("dve", lambda e: e.tensor_tensor(out=mx, in0=cc, in1=mp, op=ALU.max), r=[gk + "cc", mpk], w=[gk + "Mx"])
            A("dve", lambda e: e.tensor_tensor(out=mn, in0=mx, in1=sp[:, 8:12], op=ALU.subtract), r=[gk + "sp", gk + "Mx"], w=[mnk])
            A("dve", lambda e: e.tensor_tensor(out=ar[:, 0:4], in0=ap_, in1=mx, op=ALU.subtract), r=[gk + "apr", gk + "Mx"], w=[gk + "args"])
            A("dve", lambda e: e.tensor_tensor(out=ar[:, 4:8], in0=mp, in1=mx, op=ALU.subtract), r=[mpk, gk + "Mx"], w=[gk + "args"])
            A("dve", lambda e: e.tensor_tensor(out=ar[:, 8:12], in0=sp[:, 0:4], in1=mx, op=ALU.subtract), r=[gk + "sp", gk + "Mx"], w=[gk + "args"])
            A("act", lambda e: e.activation(out=ex, in_=ar, func=AF.Exp), r=[gk + "args"], w=[gk + "exs"])

        def H_(t, par):
            tp = t % 2
            t3 = t % 3
            sp_ = (t // QT) % 2
            mix = mix2[sp_]
            sub = t % QT
            vb = vext[t3]
            vk = f"vext{t3}"
            ex = exs[tp]
            ek = f"g{tp}exs"
            for h in (par, par + 2):
                b = h % 2
                bank = 1 if b == 0 else 7
                bk = f"B{bank}"
                hs = slice(h * 128, (h + 1) * 128)
                A("dve", lambda e, h=h, b=b, hs=hs: e.tensor_scalar(out=ke[b], in0=k_tm[t3][:, hs], scalar1=ex[:, h:h + 1], scalar2=None, op0=ALU.mult),
                  r=[f"k_tm{t3}"], sr=[ek], w=[f"ke{b}"])
                A("pe", lambda e, b=b, bank=bank: e.transpose(out=pbb(bank)[:, 776:904], in_=ke[b], identity=identb), r=[f"ke{b}", "identb"], w=[bk])
                A("dve", lambda e, b=b, bank=bank: e.tensor_copy(out=keT[b], in_=pbb(bank)[:, 776:904]), r=[bk], w=[f"keT{b}"])
                A("pe", lambda e, b=b, bank=bank, h=h: e.matmul(pb(bank)[:, 0:128], lhsT=keT[b], rhs=qT4[t3][:, h, :], start=True, stop=True), r=[f"keT{b}", f"qT4{t3}"], w=[bk])
                A("dve", lambda e, b=b, bank=bank: e.tensor_tensor(out=PTm[b], in0=pb(bank)[:, 0:128], in1=tri, op=ALU.mult), r=[bk, "cf"], w=[f"PTm{b}"])
                A("dve", lambda e, h=h: e.tensor_scalar(out=Cst[:, h, :], in0=Cst[:, h, :], scalar1=ex[:, 4 + h:5 + h], scalar2=None, op0=ALU.mult), r=[f"C{h}"], sr=[ek], w=[f"C{h}"])
                A("pool", lambda e, h=h, b=b: e.tensor_copy(out=Cb[b], in_=Cst[:, h, :]), r=[f"C{h}"], w=[f"Cb{b}"])
                A("pe", lambda e, b=b, bank=bank, h=h: e.matmul(pb(bank)[:, 128:257], lhsT=PTm[b], rhs=vb[:, h, :], start=True, stop=False), r=[f"PTm{b}", vk], w=[bk])
                A("pe", lambda e, b=b, bank=bank, h=h: e.matmul(pb(bank)[:, 128:257], lhsT=qT4[t3][:, h, :], rhs=Cb[b], start=False, stop=True), r=[f"qT4{t3}", f"Cb{b}"], w=[bk])
                A("pe", lambda e, b=b, bank=bank, h=h: e.matmul(pb(bank)[:, 258:387], lhsT=ke[b], rhs=vb[:, h, :], start=True, stop=True), r=[f"ke{b}", vk], w=[bk])
                nd = pb(bank)[:, 128:257]
                smb = sm[b]
                smk = f"sm{b}"
                A("act", lambda e, nd=nd, smb=smb: e.activation(out=smb[:, 0:1], in_=nd[:, 128:129], func=AF.Abs), r=[bk], w=[smk])
                A("act", lambda e, nd=nd, smb=smb, b=b: e.activation(out=junk2[:, 0:128], in_=nd[:, 0:128], func=AF.Square, scale=float(128.0 ** -0.5), accum_out=smb[:, 2:3]),
                  r=[bk], w=["junk2", smk])
                A("dve", lambda e, bank=bank, h=h: e.tensor_tensor(out=Cst[:, h, :], in0=pb(bank)[:, 258:387], in1=Cst[:, h, :], op=ALU.add), r=[bk, f"C{h}"], w=[f"C{h}"])
                A("dve", lambda e, h=h, smb=smb: e.tensor_tensor(out=smb[:, 0:1], in0=smb[:, 0:1], in1=ex[:, 8 + h:9 + h], op=ALU.max), r=[smk, ek], w=[smk])
                A("dve", lambda e, smb=smb: e.tensor_tensor(out=smb[:, 1:2], in0=smb[:, 0:1], in1=smb[:, 0:1], op=ALU.mult), r=[smk], w=[smk])
                A("dve", lambda e, smb=smb: e.scalar_tensor_tensor(out=smb[:, 3:4], in0=smb[:, 1:2], scalar=EPS, in1=smb[:, 2:3], op0=ALU.mult, op1=ALU.add), r=[smk], w=[smk])
                A("act", lambda e, smb=smb: e.activation(out=smb[:, 4:5], in_=smb[:, 3:4], func=AF.Ln), r=[smk], w=[smk])
                A("act", lambda e, smb=smb: e.activation(out=smb[:, 6:7], in_=smb[:, 4:5], func=AF.Exp, scale=-0.5), r=[smk], w=[smk])
                A("dve", lambda e, nd=nd, hs=hs, smb=smb: e.scalar_tensor_tensor(out=mix[:, sub, hs], in0=nd[:, 0:128], scalar=smb[:, 6:7], in1=og[t3][:, hs], op0=ALU.mult, op1=ALU.mult),
                  r=[bk, f"og{t3}"], sr=[smk], w=[f"mix{sp_}_{sub}"])

        def L_(t):
            tp = t % 2
            sp_ = (t // QT) % 2
            qTall = qTall2[sp_]
            sub = t % QT
            latt = lat[tp]
            lk = f"lat{tp}"
            A("act", lambda e: e.activation(out=junk3[:, 0:384], in_=latt[:, 8:392], func=AF.Square, scale=float(384.0 ** -0.5), accum_out=ssq[:, 1:2]), r=[lk], w=["junk3", "ssq1"])
            A("act", lambda e: e.activation(out=junk3[:, 0:256], in_=latt[:, 392:648], func=AF.Square, scale=float(256.0 ** -0.5), accum_out=ssq[:, 2:3]), r=[lk], w=["junk3", "ssq1"])
            A("act", lambda e: e.activation(out=rstd2[:, 0:2], in_=ssq[:, 1:3], func=AF.Ln, bias=epst), r=["ssq1", "epst"], w=["rstd1l"])
            A("act", lambda e: e.activation(out=rstd[:, 1:3], in_=rstd2[:, 0:2], func=AF.Exp, scale=-0.5), r=["rstd1l"], w=["rstd1"])
            A("dve", lambda e: e.tensor_scalar(out=lan[:, 0:384], in0=latt[:, 8:392], scalar1=rstd[:, 1:2], scalar2=None, op0=ALU.mult), r=[lk], sr=["rstd1"], w=["lan"])
            A("dve", lambda e: e.tensor_scalar(out=lan[:, 384:640], in0=latt[:, 392:648], scalar1=rstd[:, 2:3], scalar2=None, op0=ALU.mult), r=[lk], sr=["rstd1"], w=["lan"])
            with atomic():
                for j in range(5):
                    A("pe", lambda e, j=j: e.transpose(out=pbb(0)[:, j * 128:(j + 1) * 128], in_=lan[:, j * 128:(j + 1) * 128], identity=identb), r=["lan", "identb"], w=["B0"])
                A("act", lambda e: e.activation(out=lanT.rearrange("p a b -> p (a b)"), in_=pbb(0)[:, 0:640], func=AF.Copy), r=["B0"], w=["lanT"])
            atom[0] += 1
            for (bank, c0, cn) in ((2, 0, 480), (3, 480, 288)):
                for kc in range(3):
                    A("pe", lambda e, bank=bank, c0=c0, cn=cn, kc=kc: e.matmul(pb(bank)[:, 0:cn], lhsT=lanT[:, kc, :], rhs=wqb[:, kc, c0:c0 + cn], start=(kc == 0), stop=(kc == 2)),
                      r=["lanT", f"wqb{kc}"], w=[f"B{bank}"])
            cs = cosT[:, t, :]
            sn = sinT[:, t, :]
            for (bank, h0, nh) in ((2, 0, 5), (3, 5, 3)):
                v = pb(bank)[:, 0:nh * 96].rearrange("p (h c) -> p h c", h=nh)
                bk = f"B{bank}"
                csb = cs.unsqueeze(1).to_broadcast([128, nh, 16])
                snb = sn.unsqueeze(1).to_broadcast([128, nh, 16])
                A("act", lambda e, v=v, h0=h0, nh=nh: e.activation(out=qtm[:, h0:h0 + nh, 0:64], in_=v[:, :, 0:64], func=AF.Copy), r=[bk], w=["qtm"])
                A("dve", lambda e, v=v, nh=nh, csb=csb: e.tensor_tensor(out=rt[0][:, 0:nh, :], in0=v[:, :, 64:80], in1=csb, op=ALU.mult), r=[bk, "cosT"], w=["rt0"])
                A("dve", lambda e, v=v, nh=nh, snb=snb: e.tensor_tensor(out=rt[1][:, 0:nh, :], in0=v[:, :, 80:96], in1=snb, op=ALU.mult), r=[bk, "sinT"], w=["rt1"])
                A("dve", lambda e, v=v, nh=nh, csb=csb: e.tensor_tensor(out=rt[2][:, 0:nh, :], in0=v[:, :, 80:96], in1=csb, op=ALU.mult), r=[bk, "cosT"], w=["rt2"])
                A("dve", lambda e, v=v, nh=nh, snb=snb: e.tensor_tensor(out=rt[3][:, 0:nh, :], in0=v[:, :, 64:80], in1=snb, op=ALU.mult), r=[bk, "sinT"], w=["rt3"])
                A("pool", lambda e, h0=h0, nh=nh: e.tensor_tensor(out=qtm[:, h0:h0 + nh, 64:80], in0=rt[0][:, 0:nh, :], in1=rt[1][:, 0:nh, :], op=ALU.subtract), r=["rt0", "rt1"], w=["qtm"])
                A("pool", lambda e, h0=h0, nh=nh: e.tensor_tensor(out=qtm[:, h0:h0 + nh, 80:96], in0=rt[2][:, 0:nh, :], in1=rt[3][:, 0:nh, :], op=ALU.add), r=["rt2", "rt3"], w=["qtm"])
            atom[0] -= 1
            if cur[0] and cur[0][-1][0] == "open":
                cur[0][-1][0] = "seg"
            kr = latt[:, 648:680]
            A("pool", lambda e: e.tensor_tensor(out=rk[0], in0=kr[:, 0:16], in1=cs, op=ALU.mult), r=[lk, "cosT"], w=["rk0"])
            A("pool", lambda e: e.tensor_tensor(out=rk[1], in0=kr[:, 16:32], in1=sn, op=ALU.mult), r=[lk, "sinT"], w=["rk1"])
            A("pool", lambda e: e.tensor_tensor(out=rk[2], in0=kr[:, 16:32], in1=cs, op=ALU.mult), r=[lk, "cosT"], w=["rk2"])
            A("pool", lambda e: e.tensor_tensor(out=rk[3], in0=kr[:, 0:16], in1=sn, op=ALU.mult), r=[lk, "sinT"], w=["rk3"])
            A("pool", lambda e: e.tensor_tensor(out=kro[:, 0:16], in0=rk[0], in1=rk[1], op=ALU.subtract), r=["rk0", "rk1"], w=["kro"])
            A("pool", lambda e: e.tensor_tensor(out=kro[:, 16:32], in0=rk[2], in1=rk[3], op=ALU.add), r=["rk2", "rk3"], w=["kro"])
            A("pool", lambda e: e.tensor_copy(out=ktm[:, :, 64:96], in_=kro.unsqueeze(1).to_broadcast([128, 8, 32])), r=["kro"], w=["ktm"])
            for (bank, h0) in ((2, 0), (3, 4)):
              with atomic():
                for kc in range(2):
                    A("pe", lambda e, bank=bank, h0=h0, kc=kc: e.matmul(pb(bank), lhsT=lanT[:, 3 + kc, :], rhs=wkvb[:, kc, h0 * 128:h0 * 128 + 512], start=(kc == 0), stop=(kc == 1)),
                      r=["lanT", f"wkvb{kc}"], w=[f"B{bank}"])
                v = pb(bank).rearrange("p (h c) -> p h c", h=4)
                A("act", lambda e, v=v, h0=h0: e.activation(out=ktm[:, h0:h0 + 4, 0:64], in_=v[:, :, 0:64], func=AF.Copy), r=[f"B{bank}"], w=["ktm"])
                A("dve", lambda e, v=v, h0=h0: e.tensor_copy(out=vc[:, t, h0:h0 + 4, 0:64], in_=v[:, :, 64:128]), r=[f"B{bank}"], w=[f"vc{t}"])
            for g in range(2):
              with atomic():
                for j in range(4):
                    h = g * 4 + j
                    A("pe", lambda e, h=h, j=j: e.transpose(out=pbb(0)[0:96, j * 128:(j + 1) * 128], in_=qtm[:, h, :], identity=identb), r=["qtm", "identb"], w=["B0"])
                    A("pe", lambda e, h=h, j=j: e.transpose(out=pbb(0)[0:96, 512 + j * 128:512 + (j + 1) * 128], in_=ktm[:, h, :], identity=identb), r=["ktm", "identb"], w=["B0"])
                A("act", lambda e, g=g: e.activation(out=qTall[0:96, g * 4:g * 4 + 4, sub * 128:(sub + 1) * 128], in_=pbb(0)[0:96, 0:512].rearrange("p (a b) -> p a b", a=4), func=AF.Copy),
                  r=["B0"], w=[f"qTall{sp_}"])
                A("dve", lambda e, g=g: e.tensor_copy(out=kT[0:96, g * 4:g * 4 + 4, t * 128:(t + 1) * 128], in_=pbb(0)[0:96, 512:1024].rearrange("p (a b) -> p a b", a=4)),
                  r=["B0"], w=[f"kT{t}"])

        def attention(stile):
            sp_ = stile % 2
            mix = mix2[sp_]
            qTall = qTall2[sp_]
            npair = stile + 1
            for h in range(8):
                pend = []
                for p in range(npair):
                    sb_ = 4 + (p % 2)
                    bk = f"B{sb_}"
                    pk = p % 3
                    diag = (p == npair - 1)
                    for i in range(2):
                        kt = 2 * p + i
                        q0 = 128 if (diag and i == 1) else 0
                        A("pe", lambda e, sb_=sb_, kt=kt, q0=q0, h=h, i=i: e.matmul(pb(sb_)[:, i * SQ + q0:(i + 1) * SQ], lhsT=kT[0:96, h, kt * 128:(kt + 1) * 128], rhs=qTall[0:96, h, q0:SQ], start=True, stop=True),
                          r=[f"kT{kt}", f"qTall{sp_}"], w=[bk])
                    if not diag:
                        A("act", lambda e, sb_=sb_, pk=pk: e.activation(out=PT[pk], in_=pb(sb_), func=AF.Exp, scale=SC_ATT), r=[bk], w=[f"PT{pk}"])
                    else:
                        A("act", lambda e, sb_=sb_, pk=pk: e.activation(out=PT[pk][:, 0:SQ], in_=pb(sb_)[:, 0:SQ], func=AF.Exp, scale=SC_ATT), r=[bk], w=[f"PT{pk}"])
                        A("act", lambda e, sb_=sb_, pk=pk: e.activation(out=PT[pk][:, SQ + 128:2 * SQ], in_=pb(sb_)[:, SQ + 128:2 * SQ], func=AF.Exp, scale=SC_ATT), r=[bk], w=[f"PT{pk}"])
                        A("pool", lambda e, pk=pk: e.tensor_tensor(out=PT[pk][:, 0:128], in0=PT[pk][:, 0:128], in1=tri_b, op=ALU.mult), r=[f"PT{pk}", "tri_b"], w=[f"PT{pk}"])
                        A("pool", lambda e, pk=pk: e.tensor_tensor(out=PT[pk][:, SQ + 128:2 * SQ], in0=PT[pk][:, SQ + 128:2 * SQ], in1=tri_b, op=ALU.mult), r=[f"PT{pk}", "tri_b"], w=[f"PT{pk}"])
                    while len(pend) > 2:
                        pend.pop(0)()
                    for i in range(2):
                        kt = 2 * p + i
                        q0 = 128 if (diag and i == 1) else 0
                        pend.append(lambda kt=kt, pk=pk, q0=q0, h=h, i=i: A("pe", lambda e: e.matmul(pb(6)[0:65, q0:SQ], lhsT=vc[:, kt, h, :], rhs=PT[pk][:, i * SQ + q0:(i + 1) * SQ], start=(kt == 0), stop=(kt == 2 * npair - 1)),
                                                                        r=[f"vc{kt}", f"PT{pk}"], w=["B6"]))
                for f_ in pend:
                    f_()
                A("dve", lambda e: e.tensor_copy(out=oT[0:65, :], in_=pb(6)[0:65, 0:SQ]), r=["B6"], w=["oT"])
                for sub in range(QT):
                    A("pe", lambda e, sub=sub: e.transpose(out=pb(6)[:, 256 + sub * 65:256 + sub * 65 + 65], in_=oT[0:65, sub * 128:(sub + 1) * 128], identity=identf[0:65, 0:65]),
                      r=["oT", "cf"], w=["B6"])
                for sub in range(QT):
                    o = pb(6)[:, 256 + sub * 65:256 + sub * 65 + 65]
                    A("dve", lambda e, o=o: e.reciprocal(out=rd, in_=o[:, 64:65]), r=["B6"], w=["rd"])
                    A("dve", lambda e, o=o, sub=sub, h=h: e.tensor_scalar(out=mix[:, sub, 512 + h * 64:512 + (h + 1) * 64], in0=o[:, 0:64], scalar1=rd, scalar2=None, op0=ALU.mult),
                      r=["B6"], sr=["rd"], w=[f"mix{sp_}_{sub}"])

        def outproj(stile):
            sp_ = stile % 2
            mix = mix2[sp_]
            for sub in range(QT):
                t = stile * QT + sub
                dma(mix_d[t * 128:(t + 1) * 128, :], mix[:, sub, :], r=[f"mix{sp_}_{sub}"], w=[f"mixd{t}"])
                if dbg:
                    dma(dbg_d[t * 128:(t + 1) * 128, :], mix[:, sub, :], r=[f"mix{sp_}_{sub}"], w=[f"dbgd{t}"])

        def merge_n(lists):
            out = []
            for l in lists:
                out = merge(out, l) if out else list(l)
            return out

        def attn_store(stile):
            attention(stile)
            outproj(stile)

        ntile = nst * QT
        att = {}
        for j in range(ntile + 5):
            streams = []
            if j + 1 < ntile:
                feed(record(Xload, j + 1))
            if j < ntile:
                streams.append(record(P_, j))
            if 0 <= j - 1 < ntile:
                streams.append(record(G_, j - 1))
                streams.append(record(L_, j - 1))
            if 0 <= j - 2 < ntile:
                streams.append(record(H_, j - 2, 0))
                streams.append(record(H_, j - 2, 1))
            if j >= 3 and (j - 3) % 2 == 0 and (j - 3) // 2 < nst:
                st_ = (j - 3) // 2
                full = record(attn_store, st_)
                half = len(full) // 2
                att[j] = full[:half]
                att[j + 1] = full[half:]
            if j in att:
                streams.append(att.pop(j))
            streams = [x for x in streams if x]
            if streams:
                feed(merge_n(streams))
        P.barrier()

        lw_rr = [0]

        def LW(dst, src_d, nk, ncols, gcol0, key):
            npiece = (ncols + 1407) // 1408
            pc = ncols // npiece
            for kc in range(nk):
                for hf in range(npiece):
                    i = lw_rr[0] % 2
                    lw_rr[0] += 1
                    sbuf = stg2[i][:, 0:pc]
                    sk = f"stgb{i}"
                    dma(sbuf, src_d[kc * 128:(kc + 1) * 128, hf * pc:(hf + 1) * pc], w=[sk])
                    sc = None if gcol0 is None or (gcol0 == 13 and kc >= 4) else gains[:, gcol0 + kc:gcol0 + kc + 1]
                    cast(dst[:, kc, hf * pc:(hf + 1) * pc], sbuf, sc, r=[sk], w=[f"{key}{kc}"])

        def LW_out():
            LW(wout, wout_d, 8, D, 13, "wout")
            dma(fnw, fnw_d.partition_broadcast(128), w=["fnw"])

        def LW_gu():
            LW(wg, wg_d, 8, DFF, 17, "wg")
            LW(wu, wu_d, 8, DFF, 17, "wu")

        def LW_d():
            LW(wd, wd_d, 22, D, None, "wd")

        def S1_loads(g, sub):
            t = g * 4 + sub
            par = sub % 2
            dma(h1t[:, sub, :], x_d[t * 128:(t + 1) * 128, :], w=[f"h1t{sub}"])
            dma(hn2[par], mix_d[t * 128:(t + 1) * 128, :], r=[f"mixd{t}"], w=[f"hn{par}"])

        def S1p(g, par):
            tb, hb, ob0, hk_ = (1, 0, 4, "hn0") if par == 0 else (2, 3, 6, "hn1")
            hnb = hn2[par]
            mTb = mixT2[par]
            for sub in (par, par + 2):
                t = g * 4 + sub
                hk = f"h1t{sub}"
                if not (g > 0 and sub < 2):
                    S1_loads(g, sub)
                for kc in range(8):
                    A("pe", lambda e, kc=kc: e.transpose(out=pbb(tb)[:, kc * 128:(kc + 1) * 128], in_=hnb[:, kc * 128:(kc + 1) * 128], identity=identb), r=[hk_, "identb"], w=[f"B{tb}"])
                A("act", lambda e: e.activation(out=mTb.rearrange("p a b -> p (a b)"), in_=pbb(tb), func=AF.Copy), r=[f"B{tb}"], w=[f"mixT{par}"])
                for half in range(2):
                    bank = ob0 + half
                    for kc in range(8):
                        A("pe", lambda e, kc=kc, half=half, bank=bank: e.matmul(pb(bank), lhsT=mTb[:, kc, :], rhs=wout[:, kc, half * 512:(half + 1) * 512], start=(kc == 0), stop=(kc == 7)),
                          r=[f"mixT{par}", f"wout{kc}"], w=[f"B{bank}"])
                    A("dve", lambda e, half=half, bank=bank, sub=sub: e.tensor_tensor(out=h1t[:, sub, half * 512:(half + 1) * 512], in0=pb(bank), in1=h1t[:, sub, half * 512:(half + 1) * 512], op=ALU.add),
                      r=[f"B{bank}", hk], w=[hk])
                A("act", lambda e, sub=sub: e.activation(out=hnb, in_=h1t[:, sub, :], func=AF.Square, scale=float(D ** -0.5), accum_out=ssq[:, par:par + 1]), r=[hk], w=[hk_, f"ssq_{par}"])
                A("act", lambda e: e.activation(out=rstd2[:, par:par + 1], in_=ssq[:, par:par + 1], func=AF.Ln, bias=epst), r=[f"ssq_{par}"], sr=["epst"], w=[f"rstdl_{par}"])
                A("act", lambda e: e.activation(out=rstd[:, par:par + 1], in_=rstd2[:, par:par + 1], func=AF.Exp, scale=-0.5), r=[f"rstdl_{par}"], w=[f"rstd_{par}"])
                A("dve", lambda e, sub=sub: e.tensor_scalar(out=hnb, in0=h1t[:, sub, :], scalar1=rstd[:, par:par + 1], scalar2=None, op0=ALU.mult), r=[hk], sr=[f"rstd_{par}"], w=[hk_])
                for kc in range(8):
                    A("pe", lambda e, kc=kc: e.transpose(out=pbb(hb)[:, kc * 128:(kc + 1) * 128], in_=hnb[:, kc * 128:(kc + 1) * 128], identity=identb), r=[hk_, "identb"], w=[f"B{hb}"])
                A("act", lambda e, sub=sub: e.activation(out=hnT[:, :, sub * 128:(sub + 1) * 128], in_=pbb(hb).rearrange("p (a b) -> p a b", a=8), func=AF.Copy), r=[f"B{hb}"], w=[f"hnT{sub}"])

        def S2_(g):
            for fc in range(22):
                gb = 2 + 2 * (fc % 2)
                ub = gb + 1
                fs = slice(fc * 128, (fc + 1) * 128)
                for kc in range(8):
                    A("pe", lambda e, kc=kc, fs=fs, gb=gb: e.matmul(pb(gb), lhsT=wg[:, kc, fs], rhs=hnT[:, kc, :], start=(kc == 0), stop=(kc == 7)), r=[f"wg{kc}", "hnT0", "hnT1", "hnT2", "hnT3"], w=[f"B{gb}"])
                for kc in range(8):
                    A("pe", lambda e, kc=kc, fs=fs, ub=ub: e.matmul(pb(ub), lhsT=wu[:, kc, fs], rhs=hnT[:, kc, :], start=(kc == 0), stop=(kc == 7)), r=[f"wu{kc}", "hnT0", "hnT1", "hnT2", "hnT3"], w=[f"B{ub}"])
                sgb = sg[fc % 2]
                A("act", lambda e, gb=gb, sgb=sgb: e.activation(out=sgb, in_=pb(gb), func=AF.Silu), r=[f"B{gb}"], w=["sg"])
                A("dve", lambda e, ub=ub, sgb=sgb, fc=fc: e.tensor_tensor(out=actT[:, fc, :], in0=pb(ub), in1=sgb, op=ALU.mult), r=[f"B{ub}", "sg"], w=[f"actT{fc}"])

        def S3_(g):
            for sub in range(4):
                t = g * 4 + sub
                for half in range(2):
                    bank = 6 + half
                    for fc in range(22):
                        A("pe", lambda e, fc=fc, sub=sub, half=half, bank=bank: e.matmul(pb(bank), lhsT=actT[:, fc, sub * 128:(sub + 1) * 128], rhs=wd[:, fc, half * 512:(half + 1) * 512], start=(fc == 0), stop=(fc == 21)),
                          r=[f"actT{fc}", f"wd{fc}"], w=[f"B{bank}"])
                    A("dve", lambda e, sub=sub, half=half, bank=bank: e.tensor_tensor(out=h1t[:, sub, half * 512:(half + 1) * 512], in0=pb(bank), in1=h1t[:, sub, half * 512:(half + 1) * 512], op=ALU.add),
                      r=[f"B{bank}", f"h1t{sub}"], w=[f"h1t{sub}"])
                A("act", lambda e, sub=sub: e.activation(out=mixT2[1].rearrange("p a b -> p (a b)"), in_=h1t[:, sub, :], func=AF.Square, scale=float(D ** -0.5), accum_out=ssq[:, 3:4]), r=[f"h1t{sub}"], w=["mixT1", "ssq3"])
                A("act", lambda e: e.activation(out=rstd2[:, 3:4], in_=ssq[:, 3:4], func=AF.Ln, bias=epst), r=["ssq3", "epst"], w=["rstd3l"])
                A("act", lambda e: e.activation(out=rstd[:, 3:4], in_=rstd2[:, 3:4], func=AF.Exp, scale=-0.5), r=["rstd3l"], w=["rstd3"])
                A("dve", lambda e, sub=sub: e.scalar_tensor_tensor(out=h1t[:, sub, :], in0=h1t[:, sub, :], scalar=rstd[:, 3:4], in1=fnw, op0=ALU.mult, op1=ALU.mult),
                  r=[f"h1t{sub}", "fnw"], sr=["rstd3"], w=[f"h1t{sub}"])
                dma(out_d[t * 128:(t + 1) * 128, :], h1t[:, sub, :], r=[f"h1t{sub}"], w=[f"outd{t}"])
                if sub < 2 and g + 1 < S // 512:
                    S1_loads(g + 1, sub)


        if do_p2:
            feed(record(LW_out))
            for g in range(S // 512):
                s1 = merge(record(S1p, g, 0), record(S1p, g, 1))
                if g == 0:
                    feed(merge(s1, record(LW_gu)))
                    feed(merge(record(S2_, g), record(LW_d)))
                else:
                    feed(s1)
                    feed(record(S2_, g))
                feed(record(S3_, g))

        P.emit(nc, st)
    return nc


_CONSTS = None


def _consts():
    global _CONSTS
    if _CONSTS is None:
        identb = np.eye(128, dtype=np.float32).astype(ml_dtypes.bfloat16)
        identf = np.eye(128, dtype=np.float32)
        tri = np.triu(np.ones((128, 128), dtype=np.float32))
        ones = np.ones((128, 128), dtype=np.float32)
        _CONSTS = (identb, np.ascontiguousarray(np.concatenate([identf, tri, ones], axis=1)))
    return _CONSTS


def kernel(x, positions, attn_norm_w, w_in, b_gates, mlstm_norm_w, q_a_norm_w, w_q_b,
           kv_a_norm_w, w_kv_b, w_out, ffn_norm_w, w_gate, w_up, w_down, final_norm_w):
    f = lambda a: np.ascontiguousarray(np.asarray(a, dtype=np.float32))
    x = f(x)
    positions = np.asarray(positions, dtype=np.int32)
    identb, cf = _consts()
    col = lambda v, n: f(v).reshape(n, 128).T
    gains = np.ascontiguousarray(np.concatenate([
        col(attn_norm_w[0], 8), col(q_a_norm_w[0], 3), col(kv_a_norm_w[0], 2),
        col(mlstm_norm_w[0].reshape(-1), 4), col(ffn_norm_w[0], 8)], axis=1))
    shared = {
        "w_in": f(w_in[0]), "w_q_b": f(w_q_b[0]), "w_kv_b": f(w_kv_b[0]), "w_out": f(w_out[0]),
        "w_gate": f(w_gate[0]), "w_up": f(w_up[0]), "w_down": f(w_down[0]),
        "gains": gains, "final_nw": f(final_norm_w).reshape(1, D), "b_gates": f(b_gates).reshape(1, 8),
        "identb": identb, "cf": cf,
    }
    nc = build_program()
    in_maps = []
    for b in range(8):
        m = dict(shared)
        m["x"] = x[b]
        m["pos"] = np.ascontiguousarray(positions[b].reshape(NT, 128).T)
        in_maps.append(m)
    res = run_bass_kernel_spmd(nc, in_maps, core_ids=list(range(8)))
    return np.stack([np.asarray(r["out"], dtype=np.float32) for r in res.results], axis=0)
```
